# Optimizing a Trainium2 kernel written in Bass

```python
import math
import jax, jax.numpy as jnp
from jax import lax
import numpy as np

D_MODEL = 1024
BATCH = 16
SEQ = 2048
DEPTH = 2

HEAD_DIM = 64
D_MIX = D_MODEL
POOL_WIDTH = D_MIX // 4
POOL_WINDOWS = (2, 4, 8, 16)
POOL_GROUP = POOL_WIDTH // len(POOL_WINDOWS)
MOBA_WIDTH = D_MIX // 4
MOBA_HEADS = MOBA_WIDTH // HEAD_DIM
MOBA_BLOCK = 256
MOBA_TOPK = 3
MOBA_QCHUNK = 32
DIL_WIDTH = D_MIX // 4
DIL_HEADS = DIL_WIDTH // HEAD_DIM
DILATIONS = ((128, 1), (512, 4), (2048, 16))
CONV_WIDTH = D_MIX - POOL_WIDTH - MOBA_WIDTH - DIL_WIDTH
CONV_KERNEL = 31
ROPE_THETA = 500000.0
ROPE_DIMS = HEAD_DIM // 4
D_FF = 2816
RMS_EPS = 1e-6
LN_EPS = 1e-5
NEG_INF = -1e30
OFF_POOL = 0
OFF_MOBA = OFF_POOL + POOL_WIDTH
OFF_DIL = OFF_MOBA + 3 * MOBA_WIDTH
OFF_CONV = OFF_DIL + 3 * DIL_WIDTH
D_IN = OFF_CONV + 2 * CONV_WIDTH

kernel_name = "hybrid_pool_moba_dilated_conv_macaron"


def rms_norm(x, g):
    xf = x.astype(jnp.float32)
    y = xf * lax.rsqrt(jnp.mean(xf * xf, axis=-1, keepdims=True) + RMS_EPS)
    return (y * g.astype(jnp.float32)).astype(x.dtype)


def layer_norm(x, g, b):
    xf = x.astype(jnp.float32)
    mu = jnp.mean(xf, axis=-1, keepdims=True)
    var = jnp.mean(jnp.square(xf - mu), axis=-1, keepdims=True)
    y = (xf - mu) * lax.rsqrt(var + LN_EPS)
    return (y * g.astype(jnp.float32) + b.astype(jnp.float32)).astype(x.dtype)


def swiglu(x, wg, wu, wd):
    return (jax.nn.silu(x @ wg) * (x @ wu)) @ wd


def rope_tables(positions, dtype):
    inv = ROPE_THETA ** (-jnp.arange(0, ROPE_DIMS, 2, dtype=jnp.float32) / ROPE_DIMS)
    ang = positions.astype(jnp.float32)[..., None] * inv
    return (jnp.cos(ang)[:, :, None, :].astype(dtype), jnp.sin(ang)[:, :, None, :].astype(dtype))


def apply_rope(t, cos, sin):
    half = ROPE_DIMS // 2
    t1 = t[..., :half]
    t2 = t[..., half:ROPE_DIMS]
    return jnp.concatenate([t1 * cos - t2 * sin, t2 * cos + t1 * sin, t[..., ROPE_DIMS:]], axis=-1)


def pool_mixer(u, w, scale):
    B, S, _ = u.shape
    ug = u.reshape(B, S, len(POOL_WINDOWS), POOL_GROUP)
    t_count = jnp.arange(S, dtype=jnp.float32) + 1.0
    outs = []
    for g, wnd in enumerate(POOL_WINDOWS):
        ch = ug[:, :, g].astype(jnp.float32)
        c = jnp.cumsum(ch, axis=1)
        c_back = jnp.pad(c, ((0, 0), (wnd, 0), (0, 0)))[:, :S]
        cnt = jnp.minimum(t_count, float(wnd))[None, :, None]
        outs.append(((c - c_back) / cnt - ch).astype(u.dtype))
    pooled = jnp.stack(outs, axis=2)
    mixed = jnp.einsum('bsgc,gcd->bsgd', pooled, w)
    return mixed.reshape(B, S, POOL_WIDTH) * scale


def moba_attention(q, k, v):
    B, H, S, dh = q.shape
    L = MOBA_BLOCK
    Sp = -(-S // L) * L
    pad = ((0, 0), (0, 0), (0, Sp - S), (0, 0))
    q, k, v = jnp.pad(q, pad), jnp.pad(k, pad), jnp.pad(v, pad)
    NB = Sp // L
    kb = k.reshape(B, H, NB, L, dh)
    vb = v.reshape(B, H, NB, L, dh)
    kmean = jnp.mean(kb.astype(jnp.float32), axis=3)
    gate = jnp.einsum('bhsd,bhnd->bhsn', q.astype(jnp.float32), kmean)
    qblk = jnp.arange(Sp) // L
    past = jnp.arange(NB)[None, :] < qblk[:, None]
    gate = jnp.where(past, gate, NEG_INF)
    topk = min(MOBA_TOPK, NB)
    _, idx = lax.top_k(gate, topk)
    QC = MOBA_QCHUNK
    nc = Sp // QC
    qc = q.reshape(B, H, nc, QC, dh).transpose(2, 0, 1, 3, 4)
    ic = idx.reshape(B, H, nc, QC, topk).transpose(2, 0, 1, 3, 4)
    gather = jax.vmap(jax.vmap(lambda blocks, ii: blocks[ii]))
    scale = HEAD_DIM ** -0.5

    def chunk(args):
        c, qi, ii = args
        start = c * QC
        blk = start // L
        t = start + jnp.arange(QC)
        k_own = lax.dynamic_index_in_dim(kb, blk, axis=2, keepdims=False)
        v_own = lax.dynamic_index_in_dim(vb, blk, axis=2, keepdims=False)
        s_own = jnp.einsum('bhqd,bhkd->bhqk', qi, k_own).astype(jnp.float32) * scale
        kpos = blk * L + jnp.arange(L)
        s_own = jnp.where(kpos[None, :] <= t[:, None], s_own, NEG_INF)
        k_sel = gather(kb, ii)
        v_sel = gather(vb, ii)
        s_sel = jnp.einsum('bhqd,bhqnkd->bhqnk', qi, k_sel).astype(jnp.float32) * scale
        valid = jnp.arange(topk) < blk
        s_sel = jnp.where(valid[:, None], s_sel, NEG_INF)
        logits = jnp.concatenate([s_sel.reshape(B, H, QC, topk * L), s_own], axis=-1)
        p = jax.nn.softmax(logits, axis=-1).astype(qi.dtype)
        p_sel = p[..., :topk * L].reshape(B, H, QC, topk, L)
        p_own = p[..., topk * L:]
        return (jnp.einsum('bhqnk,bhqnkd->bhqd', p_sel, v_sel)
                + jnp.einsum('bhqk,bhkd->bhqd', p_own, v_own))

    out = lax.map(chunk, (jnp.arange(nc), qc, ic))
    out = out.transpose(1, 2, 0, 3, 4).reshape(B, H, Sp, dh)
    return out[:, :, :S]


def dilated_branch(q, k, v, window, dil):
    B, H, S, dh = q.shape
    n = S // dil
    bw = window // dil
    n_p = -(-n // bw) * bw
    nb = n_p // bw

    def to_sub(t):
        t = t.reshape(B, H, n, dil, dh).transpose(0, 1, 3, 2, 4)
        t = jnp.pad(t, ((0, 0), (0, 0), (0, 0), (0, n_p - n), (0, 0)))
        return t.reshape(B, H, dil, nb, bw, dh)

    def with_prev(t):
        prev = jnp.pad(t, ((0, 0), (0, 0), (0, 0), (1, 0), (0, 0), (0, 0)))[:, :, :, :nb]
        return jnp.concatenate([prev, t], axis=4)

    qs = to_sub(q)
    kk = with_prev(to_sub(k))
    vv = with_prev(to_sub(v))
    logits = jnp.einsum('bhrnqd,bhrnkd->bhrnqk', qs, kk).astype(jnp.float32) * (HEAD_DIM ** -0.5)
    qi = jnp.arange(bw)[:, None]
    kj = jnp.arange(2 * bw)[None, :]
    rel = qi + bw - kj
    band = (rel >= 0) & (rel <= bw)
    has_prev = (jnp.arange(nb)[:, None, None] > 0) | (kj[None] >= bw)
    mask = band[None] & has_prev
    logits = jnp.where(mask, logits, NEG_INF)
    m = jnp.max(logits, axis=-1, keepdims=True)
    e = jnp.exp(logits - m)
    den = jnp.sum(e, axis=-1, keepdims=True)
    lse = (m + jnp.log(den))[..., 0]
    p = (e / den).astype(v.dtype)
    out = jnp.einsum('bhrnqk,bhrnkd->bhrnqd', p, vv)
    out = out.reshape(B, H, dil, n_p, dh)[:, :, :, :n].transpose(0, 1, 3, 2, 4).reshape(B, H, S, dh)
    lse = lse.reshape(B, H, dil, n_p)[:, :, :, :n].transpose(0, 1, 3, 2).reshape(B, H, S)
    return out, lse


def dilated_attention(q, k, v):
    outs, lses = [], []
    for window, dil in DILATIONS:
        o, l = dilated_branch(q, k, v, window, dil)
        outs.append(o)
        lses.append(l)
    w = jax.nn.softmax(jnp.stack(lses, axis=0), axis=0)
    o = jnp.stack(outs, axis=0).astype(jnp.float32)
    return jnp.sum(w[..., None] * o, axis=0).astype(q.dtype)


def conv_module(u, conv_w, conv_b, ln_g, ln_b):
    a, g = jnp.split(u, 2, axis=-1)
    h = a * jax.nn.sigmoid(g)
    y = lax.conv_general_dilated(h, conv_w[:, None, :].astype(h.dtype), window_strides=(1,),
                                 padding=((CONV_KERNEL - 1, 0),),
                                 dimension_numbers=('NWC', 'WIO', 'NWC'),
                                 feature_group_count=CONV_WIDTH)
    y = y + conv_b
    return jax.nn.silu(layer_norm(y, ln_g, ln_b))


def to_heads(t, h):
    B, S, _ = t.shape
    return t.reshape(B, S, h, HEAD_DIM)


def setup_inputs(seed: int = 0) -> dict:
    key = jax.random.key(seed)
    ks = jax.random.split(key, 24)
    f32 = jnp.float32

    def nrm(k, shape, fan_in):
        return jax.random.normal(k, shape, f32) * (fan_in ** -0.5)

    def gain(k, shape):
        return 1.0 + 0.05 * jax.random.normal(k, shape, f32)

    x = jax.random.normal(ks[0], (BATCH, SEQ, D_MODEL), f32)
    positions = jnp.arange(SEQ, dtype=jnp.int32)[None, :] + jax.random.randint(ks[1], (BATCH, 1), 0, 4096, jnp.int32)
    return {
        "x": x,
        "positions": positions,
        "ffn1_norm": gain(ks[2], (DEPTH, D_MODEL)),
        "ffn1_gate": nrm(ks[3], (DEPTH, D_MODEL, D_FF), D_MODEL),
        "ffn1_up": nrm(ks[4], (DEPTH, D_MODEL, D_FF), D_MODEL),
        "ffn1_down": nrm(ks[5], (DEPTH, D_FF, D_MODEL), D_FF),
        "mix_norm": gain(ks[6], (DEPTH, D_MODEL)),
        "w_in": nrm(ks[7], (DEPTH, D_MODEL, D_IN), D_MODEL),
        "pool_w": nrm(ks[8], (DEPTH, len(POOL_WINDOWS), POOL_GROUP, POOL_GROUP), POOL_GROUP),
        "pool_scale": 1.0 + 0.1 * jax.random.normal(ks[9], (DEPTH, POOL_WIDTH), f32),
        "conv_w": nrm(ks[10], (DEPTH, CONV_KERNEL, CONV_WIDTH), CONV_KERNEL),
        "conv_b": 0.01 * jax.random.normal(ks[11], (DEPTH, CONV_WIDTH), f32),
        "conv_ln_g": gain(ks[12], (DEPTH, CONV_WIDTH)),
        "conv_ln_b": 0.01 * jax.random.normal(ks[13], (DEPTH, CONV_WIDTH), f32),
        "w_out": nrm(ks[14], (DEPTH, D_MIX, D_MODEL), D_MIX),
        "ffn2_norm": gain(ks[15], (DEPTH, D_MODEL)),
        "ffn2_gate": nrm(ks[16], (DEPTH, D_MODEL, D_FF), D_MODEL),
        "ffn2_up": nrm(ks[17], (DEPTH, D_MODEL, D_FF), D_MODEL),
        "ffn2_down": nrm(ks[18], (DEPTH, D_FF, D_MODEL), D_FF),
        "final_norm": gain(ks[19], (D_MODEL,)),
    }


def reference(x, positions, ffn1_norm, ffn1_gate, ffn1_up, ffn1_down, mix_norm, w_in, pool_w, pool_scale,
              conv_w, conv_b, conv_ln_g, conv_ln_b, w_out, ffn2_norm, ffn2_gate, ffn2_up, ffn2_down, final_norm):
    cos, sin = rope_tables(positions, x.dtype)
    for l in range(DEPTH):
        x = x + 0.5 * swiglu(rms_norm(x, ffn1_norm[l]), ffn1_gate[l], ffn1_up[l], ffn1_down[l])
        h = rms_norm(x, mix_norm[l]) @ w_in[l]
        u_pool = h[..., OFF_POOL:OFF_MOBA]
        qm, km, vm = jnp.split(h[..., OFF_MOBA:OFF_DIL], 3, axis=-1)
        qd, kd, vd = jnp.split(h[..., OFF_DIL:OFF_CONV], 3, axis=-1)
        u_conv = h[..., OFF_CONV:]
        B, S, _ = x.shape
        y_pool = pool_mixer(u_pool, pool_w[l], pool_scale[l])
        qm = apply_rope(to_heads(qm, MOBA_HEADS), cos, sin).transpose(0, 2, 1, 3)
        km = apply_rope(to_heads(km, MOBA_HEADS), cos, sin).transpose(0, 2, 1, 3)
        vm = to_heads(vm, MOBA_HEADS).transpose(0, 2, 1, 3)
        y_moba = moba_attention(qm, km, vm).transpose(0, 2, 1, 3).reshape(B, S, MOBA_WIDTH)
        qd = apply_rope(to_heads(qd, DIL_HEADS), cos, sin).transpose(0, 2, 1, 3)
        kd = apply_rope(to_heads(kd, DIL_HEADS), cos, sin).transpose(0, 2, 1, 3)
        vd = to_heads(vd, DIL_HEADS).transpose(0, 2, 1, 3)
        y_dil = dilated_attention(qd, kd, vd).transpose(0, 2, 1, 3).reshape(B, S, DIL_WIDTH)
        y_conv = conv_module(u_conv, conv_w[l], conv_b[l], conv_ln_g[l], conv_ln_b[l])
        mix = jnp.concatenate([y_pool, y_moba, y_dil, y_conv], axis=-1)
        x = x + mix @ w_out[l]
        x = x + 0.5 * swiglu(rms_norm(x, ffn2_norm[l]), ffn2_gate[l], ffn2_up[l], ffn2_down[l])
    return rms_norm(x, final_norm)
```

```python
import numpy as np
from contextlib import ExitStack
import concourse.bass as bass
import concourse.mybir as mybir
from concourse.bass_utils import run_bass_kernel_spmd

F32 = mybir.dt.float32
BF16 = mybir.dt.bfloat16
I32 = mybir.dt.int32
ALU = mybir.AluOpType
AF = mybir.ActivationFunctionType
AX = mybir.AxisListType

D = 1024
S = 2048
FF = 2816
NL = 2
KC = 8
FC = 22
TT = 512
NT = 4
NSEQ = 2
NCORES = 8
D_IN = 2304
OFF_POOL, OFF_MOBA, OFF_DIL, OFF_CONV = 0, 256, 1024, 1792
NEG = -30000.0
GRAN = 1024
DBG = {"attn_level": 9}

PV_N1, PV_NM, PV_N2, PV_NF = 0, 16, 32, 48
PV_PS = 56
PV_CW = 60
PV_CB, PV_LG, PV_LB = 184, 188, 192
PV_INV = 196
PV_PIW = 197
PV_EPS1, PV_EPS2 = 200, 201
PV_NCOL = 204


class Sched:
    ENG = ("pe", "act", "dve", "pool", "sp")
    EPOCH = 1000

    def __init__(self):
        self.ops = []
        self.last_w = {}
        self.readers = {}

    @staticmethod
    def keys(ap):
        name = ap.name
        pat = ap.ap
        es = mybir.dt.size(ap.dtype)
        pstride = pat[0][0]
        off = ap.offset % pstride if pstride > 0 else ap.offset
        dims = [d for d in pat[1:] if d[1] > 1]
        if not dims:
            dims = [(1, 1)]
        last = dims[-1]
        run = (last[1] - 1) * abs(last[0]) + 1
        outer = dims[:-1]
        ks = set()
        idx = [0] * len(outer)
        while True:
            o = off + sum(i * d[0] for i, d in zip(idx, outer))
            lo = o * es
            hi = (o + run) * es - 1
            for g in range(lo // GRAN, hi // GRAN + 1):
                ks.add((name, g))
            j = len(outer) - 1
            while j >= 0:
                idx[j] += 1
                if idx[j] < outer[j][1]:
                    break
                idx[j] = 0
                j -= 1
            if j < 0:
                break
        return ks

    def op(self, eng, fn, r=(), w=(), dma=False, out_dma=False):
        deps = set()
        rk = set()
        wk = set()
        for a in r:
            rk |= self.keys(a)
        for a in w:
            wk |= self.keys(a)
        for k in rk:
            if k in self.last_w:
                deps.add(self.last_w[k])
        for k in wk:
            if k in self.last_w:
                deps.add(self.last_w[k])
            rd = self.readers.get(k)
            if rd:
                deps.update(rd.values())
        i = len(self.ops)
        self.ops.append(dict(eng=eng, fn=fn, deps=deps, dma=dma, out=out_dma))
        tag = ("dma", i) if dma else eng
        for k in rk:
            self.readers.setdefault(k, {})[tag] = i
        for k in wk:
            self.last_w[k] = i
            self.readers[k] = {}
        return i

    def emit(self, nc, es):
        ops = self.ops
        signal = [False] * len(ops)
        for o in ops:
            for d in o["deps"]:
                if not ops[d]["dma"]:
                    if not (ops[d]["eng"] == "pe" and o["eng"] == "pe" and not o["dma"]):
                        signal[d] = True
        cnt = {e: 0 for e in self.ENG}
        sigcnt = [0] * len(ops)
        for i, o in enumerate(ops):
            if signal[i]:
                cnt[o["eng"]] += 1
                sigcnt[i] = cnt[o["eng"]]
        sems = {}
        for e in self.ENG:
            n = cnt[e] // self.EPOCH + 1
            sems[e] = [es.enter_context(nc.semaphore("s_%s_%d" % (e, j))) for j in range(n)]
        NDS = 12
        dsem = {e: [es.enter_context(nc.semaphore("d_%s_%d" % (e, j))) for j in range(NDS)]
                for e in ("sp", "pool", "act")}
        dcount = {e: [0] * NDS for e in dsem}
        dnext = {e: 0 for e in dsem}
        dma_tok = {}
        out_dmas = []
        per_eng = {e: [] for e in self.ENG}
        for i, o in enumerate(ops):
            per_eng[o["eng"]].append(i)
        for i, o in enumerate(ops):
            if o["dma"]:
                e = o["eng"]
                k = dnext[e]
                dnext[e] = (k + 1) % NDS
                prev = dcount[e][k]
                dcount[e][k] += 1
                dma_tok[i] = (dsem[e][k], 16 * (prev + 1), 16 * prev)
                if o["out"]:
                    out_dmas.append(i)
        EP = self.EPOCH
        if DBG.get('verbose'):
            print('signals', cnt, 'dma per sem', {e: max(v) for e, v in dcount.items()}, flush=True)

        def run_engine(ename, e):
            waited = {f: 0 for f in self.ENG}
            dwaited = {}
            for i in per_eng[ename]:
                o = ops[i]
                need = {}
                for d in o["deps"]:
                    od = ops[d]
                    if od["dma"]:
                        sem, val, _ = dma_tok[d]
                        if dwaited.get(sem.num if hasattr(sem, "num") else id(sem), 0) < val:
                            dwaited[sem.num if hasattr(sem, "num") else id(sem)] = val
                            e.wait_ge(sem, val)
                    else:
                        f = od["eng"]
                        if f == "pe" and ename == "pe" and not o["dma"]:
                            continue
                        c = sigcnt[d]
                        if c > need.get(f, 0):
                            need[f] = c
                for f, c in need.items():
                    if c > waited[f]:
                        waited[f] = c
                        e.wait_ge(sems[f][(c - 1) // EP], (c - 1) % EP + 1)
                if o["dma"]:
                    sem, val, prev = dma_tok[i]
                    key = sem.num if hasattr(sem, "num") else id(sem)
                    if prev > 0 and dwaited.get(key, 0) < prev:
                        dwaited[key] = prev
                        e.wait_ge(sem, prev)
                    ins = o["fn"](e)
                    ins.then_inc(sem, 16)
                else:
                    ins = o["fn"](e)
                    if signal[i]:
                        c = sigcnt[i]
                        ins.then_inc(sems[ename][(c - 1) // EP], 1)
            if ename == "sp":
                for i in out_dmas:
                    sem, val, _ = dma_tok[i]
                    e.wait_ge(sem, val)

        block = es.enter_context(nc.Block())

        @block.tensor
        def _(e):
            run_engine("pe", e)

        @block.scalar
        def _(e):
            run_engine("act", e)

        @block.vector
        def _(e):
            run_engine("dve", e)

        @block.gpsimd
        def _(e):
            run_engine("pool", e)

        @block.sync
        def _(e):
            run_engine("sp", e)


def build_program(nseq=NSEQ, nlayers=NL, stages=("ffn1", "mix", "ffn2"), final_norm=True,
                  mix_parts=("pool", "conv", "moba", "dil")):
    nc = bass.Bass("TRN2", target_bir_lowering=False)
    es = ExitStack()

    def din(name, shape, dt=F32):
        return nc.dram_tensor(name, list(shape), dt, kind="ExternalInput").ap()

    xT = din("xT", [nseq, D, S])
    posb = din("posb", [nseq, 128, S], I32)
    wg = [din("ffn1_gate", [NL, D, FF]), din("ffn2_gate", [NL, D, FF])]
    wu = [din("ffn1_up", [NL, D, FF]), din("ffn2_up", [NL, D, FF])]
    wd = [din("ffn1_down", [NL, FF, D]), din("ffn2_down", [NL, FF, D])]
    w_in = din("w_in", [NL, D, D_IN])
    w_out = din("w_out", [NL, D, D])
    pool_w = din("pool_w", [NL, 4, 64, 64])
    pvec_d = din("pvec", [128, PV_NCOL])
    ident_d = din("ident", [128, 128])
    tri_d = din("tri", [128, 128])
    dmask_d = din("dmask", [128, S])
    negm_d = din("negm", [128, 128])
    esel_d = din("esel", [8, 8 * 128])
    oneh_d = din("oneh", [128, 256])
    icnt_d = din("icnt", [128, 32])
    outT = nc.dram_tensor("outT", [nseq, D, S], F32, kind="ExternalOutput").ap()

    ARENA_BYTES = 207 * 1024
    A = es.enter_context(nc.sbuf_tensor("A", [128, ARENA_BYTES // 4], F32))
    banks = [es.enter_context(nc.psum_tensor("P%d" % i, [128, 512], F32)) for i in range(8)]

    cur = [0]

    def alloc(shape, dt, at=None):
        esz = mybir.dt.size(dt)
        n = int(np.prod(shape)) * esz
        n_al = (n + GRAN - 1) // GRAN * GRAN
        if at is None:
            off = cur[0]
            cur[0] += n_al
        else:
            off = at
        assert off + n <= ARENA_BYTES, ("arena overflow", off, n)
        a = A[:, off // 4:(off + n) // 4]
        if dt != F32:
            a = a.bitcast(dt)
        if len(shape) == 2:
            a = a.rearrange("p (a b) -> p a b", a=shape[0], b=shape[1])
        elif len(shape) == 3:
            a = a.rearrange("p (a b c) -> p a b c", a=shape[0], b=shape[1], c=shape[2])
        return a

    KB = 1024
    X = alloc([KC, S], F32)
    XN = alloc([KC, S], BF16)
    PV = alloc([PV_NCOL], F32)
    IDB = alloc([128], BF16)
    IDF = alloc([128], F32)
    ONES = alloc([128], BF16)
    ONEH = alloc([2, 128], BF16)
    TRI = alloc([128], BF16)
    NEGM = alloc([16, 8], F32)
    ESEL = alloc([8, 128], BF16)
    ICNT = alloc([2, 16], F32)
    GU = [alloc([2, KC, 128], BF16) for _ in range(4)]
    SQ = [alloc([TT], BF16) for _ in range(2)]
    RSTD = alloc([TT], F32)
    TMP = [alloc([TT], F32) for _ in range(4)]
    ROT = alloc([128], BF16)
    phase_base = cur[0]
    HT = alloc([11, S], BF16)
    WD = [alloc([11, 128], BF16) for _ in range(3)]
    ffn_end = cur[0]
    cur[0] = phase_base
    CT = alloc([S], F32)
    SN = alloc([S], F32)
    QK = alloc([2, S], BF16)
    VP = alloc([16, 2, 128], BF16)
    WO = [alloc([2, D], BF16) for _ in range(2)]
    YG = alloc([2, TT], BF16)
    YA = alloc([S], BF16)
    ET = [alloc([TT], BF16) for _ in range(4)]
    att_base = cur[0]
    DM = alloc([S], BF16)
    BIAST = alloc([2, S], BF16)
    GM = alloc([2, 16, 8], F32)
    MX8 = alloc([2, 16, 8], F32)
    BI = alloc([2, 16, 8], F32)
    KM = alloc([8], F32)
    KMB = alloc([8], BF16)
    att_end = cur[0]
    cur[0] = att_base
    U = alloc([2, 16 + TT], F32)
    T1 = alloc([16 + TT], F32)
    T2 = alloc([16 + TT], F32)
    PB = alloc([TT], BF16)
    FIX = alloc([16], F32)
    PW = alloc([2, 128], BF16)
    pool_end = cur[0]
    cur[0] = att_base
    HC = alloc([2, 30 + TT], F32)
    ACC = alloc([2, TT], F32)
    YB = alloc([2, TT], BF16)
    pc_end = max(cur[0], pool_end)
    cur[0] = att_base
    POSI = alloc([S], I32)
    ANG = alloc([S], F32)
    KF = CT
    rope_end = cur[0]
    assert max(ffn_end, att_end, pc_end, rope_end) <= ARENA_BYTES, (ffn_end, att_end, pc_end, rope_end)

    sc = Sched()
    op = sc.op
    psrr = [0]

    def bank():
        b = banks[psrr[0] % 6]
        psrr[0] += 1
        return b

    def pvc(col, p0=0, p1=128):
        return PV[p0:p1, col:col + 1]

    op("sp", lambda e: e.dma_start(out=PV[:, :], in_=pvec_d), w=[PV[:, :]], dma=True)
    op("sp", lambda e: e.dma_start(out=IDF[:, :], in_=ident_d), w=[IDF[:, :]], dma=True)
    op("sp", lambda e: e.dma_start(out=NEGM[:, :, :], in_=negm_d.rearrange("p (a b) -> p a b", a=16)),
       w=[NEGM[:, :, :]], dma=True)
    op("sp", lambda e: e.dma_start(out=ICNT[:, :, :], in_=icnt_d.rearrange("p (a b) -> p a b", a=2)),
       w=[ICNT[:, :, :]], dma=True)
    op("pool", lambda e: e.dma_start(out=IDB[:, :], in_=ident_d), w=[IDB[:, :]], dma=True)
    op("pool", lambda e: e.dma_start(out=TRI[:, :], in_=tri_d), w=[TRI[:, :]], dma=True)
    op("pool", lambda e: e.dma_start(out=ONEH[:, :, :], in_=oneh_d.rearrange("p (a b) -> p a b", a=2)),
       w=[ONEH[:, :, :]], dma=True)
    op("pool", lambda e: e.dma_start(out=ESEL[0:8, :, :], in_=esel_d.rearrange("p (a b) -> p a b", a=8)),
       w=[ESEL[0:8, :, :]], dma=True)
    op("dve", lambda e: e.memset(ONES[:, :], 1.0), w=[ONES[:, :]])

    def tl(t):
        return slice(t * TT, (t + 1) * TT)

    def rmsnorm_tile(gcol, t, dst):
        ps = bank()
        for c in range(KC):
            sq = SQ[c % 2]
            op("act", lambda e, c=c, sq=sq: e.activation(sq[:, :], X[:, c, tl(t)], AF.Square),
               r=[X[:, c, tl(t)]], w=[sq[:, :]])
            op("pe", lambda e, c=c, sq=sq: e.matmul(ps[:, :], ONES[:, :], sq[:, :], start=(c == 0), stop=(c == KC - 1)),
               r=[sq[:, :], ONES[:, :]], w=[ps[:, :]])
        op("act", lambda e: e.activation(RSTD[:, :], ps[:, :], AF.Sqrt, bias=pvc(PV_EPS1), scale=1.0 / D),
           r=[ps[:, :], PV[:, :]], w=[RSTD[:, :]])
        op("dve", lambda e: e.reciprocal(RSTD[:, :], RSTD[:, :]), r=[RSTD[:, :]], w=[RSTD[:, :]])
        for c in range(KC):
            op("dve", lambda e, c=c: e.scalar_tensor_tensor(dst[:, c, :], X[:, c, tl(t)], pvc(gcol + c), RSTD[:, :],
                                                           ALU.mult, ALU.mult),
               r=[X[:, c, tl(t)], RSTD[:, :], PV[:, :]], w=[dst[:, c, :]])

    def ffn(l, which):
        gcol = (PV_N1 if which == 0 else PV_N2) + l * 8
        Wg, Wu, Wd = wg[which][l], wu[which][l], wd[which][l]
        for t in range(NT):
            rmsnorm_tile(gcol, t, XN[:, :, tl(t)])
        slot = [0]
        dslot = [0]
        for half in range(2):
            for fl in range(11):
                f = half * 11 + fl
                g = GU[slot[0] % 4]
                slot[0] += 1
                for gi, Wsrc in ((0, Wg), (1, Wu)):
                    for kh in range(2):
                        op("pool", lambda e, g=g, f=f, gi=gi, Wsrc=Wsrc, kh=kh: e.dma_start(
                            out=g[:, gi, kh * 4:(kh + 1) * 4, :],
                            in_=Wsrc[kh * 512:(kh + 1) * 512, f * 128:(f + 1) * 128].rearrange("(k p) n -> p k n", p=128)),
                           w=[g[:, gi, kh * 4:(kh + 1) * 4, :]], dma=True)
                for t in range(NT):
                    pg = bank()
                    pu = bank()

                    def mmg(e, g=g, t=t, pg=pg, pu=pu):
                        for k in range(KC):
                            e.matmul(pg[:, :], g[:, 0, k, :], XN[:, k, tl(t)], start=(k == 0), stop=(k == KC - 1))
                        for k in range(KC):
                            ins = e.matmul(pu[:, :], g[:, 1, k, :], XN[:, k, tl(t)], start=(k == 0), stop=(k == KC - 1))
                        return ins
                    op("pe", mmg, r=[g[:, :, :, :], XN[:, :, tl(t)]], w=[pg[:, :], pu[:, :]])
                    sg = TMP[(f * NT + t) % 2]
                    op("act", lambda e, sg=sg, pg=pg: e.activation(sg[:, :], pg[:, :], AF.Silu),
                       r=[pg[:, :]], w=[sg[:, :]])
                    op("dve", lambda e, sg=sg, pu=pu, fl=fl, t=t: e.tensor_tensor(HT[:, fl, tl(t)], sg[:, :], pu[:, :], ALU.mult),
                       r=[sg[:, :], pu[:, :]], w=[HT[:, fl, tl(t)]])
            for d in range(KC):
                wdt = WD[dslot[0] % 3]
                dslot[0] += 1
                for f0, f1 in ((0, 4), (4, 8), (8, 11)):
                    op("pool", lambda e, wdt=wdt, d=d, half=half, f0=f0, f1=f1: e.dma_start(
                        out=wdt[:, f0:f1, :],
                        in_=Wd[(half * 11 + f0) * 128:(half * 11 + f1) * 128, d * 128:(d + 1) * 128].rearrange("(f p) n -> p f n", p=128)),
                       w=[wdt[:, f0:f1, :]], dma=True)
                for t in range(NT):
                    pa = bank()

                    def mmd(e, wdt=wdt, t=t, pa=pa):
                        for fl in range(11):
                            ins = e.matmul(pa[:, :], wdt[:, fl, :], HT[:, fl, tl(t)], start=(fl == 0), stop=(fl == 10))
                        return ins
                    op("pe", mmd, r=[wdt[:, :, :], HT[:, :, tl(t)]], w=[pa[:, :]])
                    op("dve", lambda e, pa=pa, d=d, t=t: e.scalar_tensor_tensor(
                        X[:, d, tl(t)], pa[:, :], 0.5, X[:, d, tl(t)], ALU.mult, ALU.add),
                       r=[pa[:, :], X[:, d, tl(t)]], w=[X[:, d, tl(t)]])

    def wout_apply(wo, nch, rhs_of, t):
        for d in range(KC):
            pa = bank()

            def mm(e, pa=pa, d=d):
                for c in range(nch):
                    ins = e.matmul(pa[:, :], wo[:, c, d * 128:(d + 1) * 128], rhs_of(c), start=(c == 0), stop=(c == nch - 1))
                return ins
            op("pe", mm, r=[wo[:, 0:nch, :]] + [rhs_of(c) for c in range(nch)], w=[pa[:, :]])
            op("dve", lambda e, pa=pa, d=d: e.tensor_tensor(X[:, d, tl(t)], pa[:, :], X[:, d, tl(t)], ALU.add),
               r=[pa[:, :], X[:, d, tl(t)]], w=[X[:, d, tl(t)]])

    def load_wout(l, row0, nch, wo):
        op("pool", lambda e: e.dma_start(
            out=wo[:, 0:nch, :], in_=w_out[l][row0:row0 + nch * 128, :].rearrange("(c p) n -> p c n", p=128)),
           w=[wo[:, 0:nch, :]], dma=True)

    def load_win(l, g, dst_off, col0, ncol):
        gv = g.rearrange("p a k n -> p (a k n)").rearrange("p (k n) -> p k n", k=KC)
        for kh in range(2):
            op("pool", lambda e, kh=kh: e.dma_start(
                out=gv[:, kh * 4:(kh + 1) * 4, dst_off:dst_off + ncol],
                in_=w_in[l][kh * 512:(kh + 1) * 512, col0:col0 + ncol].rearrange("(k p) n -> p k n", p=128)),
               w=[gv[:, kh * 4:(kh + 1) * 4, dst_off:dst_off + ncol]], dma=True)
        return gv

    wslot = [0]
    woslot = [0]

    def mix_pool(l):
        g = GU[wslot[0] % 4]
        wslot[0] += 1
        gv = load_win(l, g, 0, OFF_POOL, 256)
        wo = WO[woslot[0] % 2]
        woslot[0] += 1
        load_wout(l, 0, 2, wo)
        op("dve", lambda e: e.memset(PW[:, :, :], 0.0), w=[PW[:, :, :]])
        for c in range(2):
            for h in range(2):
                op("pool", lambda e, c=c, h=h: e.dma_start(out=PW[h * 64:(h + 1) * 64, c, h * 64:(h + 1) * 64],
                                                           in_=pool_w[l, 2 * c + h]),
                   w=[PW[:, c, :]], dma=True)
        op("dve", lambda e: e.memset(U[:, :, 0:16], 0.0), w=[U[:, :, 0:16]])
        W = 16 + TT

        def level(dst, src, sh, lo):
            op("dve", lambda e: e.tensor_tensor(dst[:, lo:W], src[:, lo:W], src[:, lo - sh:W - sh], ALU.add),
               r=[src[:, 0:W]], w=[dst[:, 0:W]])

        def finish(sw, c, p0, p1, t):
            op("dve", lambda e: e.scalar_tensor_tensor(
                PB[p0:p1, :], sw[p0:p1, 16:W], pvc(PV_PIW + c, p0, p1), U[p0:p1, c, 16:W], ALU.mult, ALU.subtract),
               r=[sw[:, 0:W], U[:, c, 16:W], PV[:, :]], w=[PB[:, :]])
            if t == 0:
                op("dve", lambda e: e.tensor_tensor(FIX[p0:p1, :], sw[p0:p1, 16:32], ICNT[p0:p1, c, :], ALU.mult),
                   r=[sw[:, 0:W], ICNT[:, :, :]], w=[FIX[:, :]])
                op("dve", lambda e: e.tensor_tensor(PB[p0:p1, 0:16], FIX[p0:p1, :], U[p0:p1, c, 16:32], ALU.subtract),
                   r=[FIX[:, :], U[:, c, 16:32]], w=[PB[:, :]])

        def tile_body(t):
            for c in range(2):
                ps = bank()

                def mm(e, ps=ps, c=c):
                    for k in range(KC):
                        ins = e.matmul(ps[:, :], gv[:, k, c * 128:(c + 1) * 128], XN[:, k, tl(t)], start=(k == 0), stop=(k == KC - 1))
                    return ins
                op("pe", mm, r=[gv[:, :, 0:256], XN[:, :, tl(t)]], w=[ps[:, :]])
                if t > 0:
                    op("dve", lambda e, c=c: e.tensor_copy(U[:, c, 0:16], U[:, c, TT:TT + 16]),
                       r=[U[:, c, TT:TT + 16]], w=[U[:, c, 0:16]])
                op("act", lambda e, ps=ps, c=c: e.activation(U[:, c, 16:W], ps[:, :], AF.Copy),
                   r=[ps[:, :]], w=[U[:, c, 16:W]])
                Uc = U[:, c, :]
                level(T1, Uc, 1, 1)
                if c == 0:
                    finish(T1, c, 0, 64, t)
                    level(T2, T1, 2, 3)
                    finish(T2, c, 64, 128, t)
                else:
                    level(T2, T1, 2, 3)
                    level(T1, T2, 4, 7)
                    finish(T1, c, 0, 64, t)
                    level(T2, T1, 8, 15)
                    finish(T2, c, 64, 128, t)
                ps2 = bank()
                op("pe", lambda e, ps2=ps2, c=c: e.matmul(ps2[:, :], PW[:, c, :], PB[:, :], start=True, stop=True),
                   r=[PW[:, c, :], PB[:, :]], w=[ps2[:, :]])
                op("act", lambda e, ps2=ps2, c=c: e.activation(YG[:, c, :], ps2[:, :], AF.Copy, scale=pvc(PV_PS + l * 2 + c)),
                   r=[ps2[:, :], PV[:, :]], w=[YG[:, c, :]])
            wout_apply(wo, 2, lambda c: YG[:, c, :], t)
        for t in range(NT):
            tile_body(t)

    def mix_conv(l):
        g = GU[wslot[0] % 4]
        wslot[0] += 1
        g2 = GU[wslot[0] % 4]
        wslot[0] += 1
        gva = load_win(l, g, 0, OFF_CONV, 256)
        gvg = load_win(l, g2, 0, OFF_CONV + 256, 256)
        wo = WO[woslot[0] % 2]
        woslot[0] += 1
        load_wout(l, 768, 2, wo)
        op("dve", lambda e: e.memset(HC[:, :, 0:30], 0.0), w=[HC[:, :, 0:30]])

        def tile_body(t):
            for c in range(2):
                pa = bank()
                pg = bank()

                def mm(e, pa=pa, pg=pg, c=c):
                    for k in range(KC):
                        e.matmul(pa[:, :], gva[:, k, c * 128:(c + 1) * 128], XN[:, k, tl(t)], start=(k == 0), stop=(k == KC - 1))
                    for k in range(KC):
                        ins = e.matmul(pg[:, :], gvg[:, k, c * 128:(c + 1) * 128], XN[:, k, tl(t)], start=(k == 0), stop=(k == KC - 1))
                    return ins
                op("pe", mm, r=[gva[:, :, 0:256], gvg[:, :, 0:256], XN[:, :, tl(t)]], w=[pa[:, :], pg[:, :]])
                sg = TMP[c]
                op("act", lambda e, sg=sg, pg=pg: e.activation(sg[:, :], pg[:, :], AF.Sigmoid), r=[pg[:, :]], w=[sg[:, :]])
                if t > 0:
                    op("dve", lambda e, c=c: e.tensor_copy(HC[:, c, 0:30], HC[:, c, TT:TT + 30]),
                       r=[HC[:, c, TT:TT + 30]], w=[HC[:, c, 0:30]])
                op("dve", lambda e, sg=sg, pa=pa, c=c: e.tensor_tensor(HC[:, c, 30:30 + TT], pa[:, :], sg[:, :], ALU.mult),
                   r=[pa[:, :], sg[:, :]], w=[HC[:, c, 30:30 + TT]])
                cwb = PV_CW + (l * 2 + c) * 31
                op("dve", lambda e, c=c, cwb=cwb: e.tensor_scalar(ACC[:, c, :], HC[:, c, 0:TT], pvc(cwb), pvc(PV_CB + l * 2 + c),
                                                                  ALU.mult, ALU.add),
                   r=[HC[:, c, :], PV[:, :]], w=[ACC[:, c, :]])
                for j in range(1, 31):
                    op("dve", lambda e, c=c, cwb=cwb, j=j: e.scalar_tensor_tensor(
                        ACC[:, c, :], HC[:, c, j:j + TT], pvc(cwb + j), ACC[:, c, :], ALU.mult, ALU.add),
                       r=[HC[:, c, :], ACC[:, c, :], PV[:, :]], w=[ACC[:, c, :]])
            pm = bank()
            pq = bank()
            for c in range(2):
                op("act", lambda e, c=c: e.activation(YB[:, c, :], ACC[:, c, :], AF.Copy), r=[ACC[:, c, :]], w=[YB[:, c, :]])
                op("pe", lambda e, c=c, pm=pm: e.matmul(pm[:, :], ONES[:, :], YB[:, c, :], start=(c == 0), stop=(c == 1)),
                   r=[YB[:, c, :], ONES[:, :]], w=[pm[:, :]])
            for c in range(2):
                sq = SQ[c]
                op("act", lambda e, c=c, sq=sq: e.activation(sq[:, :], ACC[:, c, :], AF.Square), r=[ACC[:, c, :]], w=[sq[:, :]])
                op("pe", lambda e, c=c, sq=sq, pq=pq: e.matmul(pq[:, :], ONES[:, :], sq[:, :], start=(c == 0), stop=(c == 1)),
                   r=[sq[:, :], ONES[:, :]], w=[pq[:, :]])
            MU, VAR = TMP[2], TMP[3]
            op("dve", lambda e, pm=pm: e.tensor_scalar(MU[:, :], pm[:, :], 1.0 / 256, None, ALU.mult), r=[pm[:, :]], w=[MU[:, :]])
            op("dve", lambda e: e.tensor_tensor(VAR[:, :], MU[:, :], MU[:, :], ALU.mult), r=[MU[:, :]], w=[VAR[:, :]])
            op("dve", lambda e, pq=pq: e.scalar_tensor_tensor(VAR[:, :], pq[:, :], 1.0 / 256, VAR[:, :], ALU.mult, ALU.subtract),
               r=[pq[:, :], VAR[:, :]], w=[VAR[:, :]])
            op("act", lambda e: e.activation(VAR[:, :], VAR[:, :], AF.Sqrt, bias=pvc(PV_EPS2), scale=1.0), r=[VAR[:, :], PV[:, :]], w=[VAR[:, :]])
            op("dve", lambda e: e.reciprocal(VAR[:, :], VAR[:, :]), r=[VAR[:, :]], w=[VAR[:, :]])
            for c in range(2):
                op("dve", lambda e, c=c: e.tensor_tensor(ACC[:, c, :], ACC[:, c, :], MU[:, :], ALU.subtract),
                   r=[ACC[:, c, :], MU[:, :]], w=[ACC[:, c, :]])
                op("dve", lambda e, c=c: e.tensor_tensor(ACC[:, c, :], ACC[:, c, :], VAR[:, :], ALU.mult),
                   r=[ACC[:, c, :], VAR[:, :]], w=[ACC[:, c, :]])
                op("act", lambda e, c=c: e.activation(YG[:, c, :], ACC[:, c, :], AF.Silu,
                                                      bias=pvc(PV_LB + l * 2 + c), scale=pvc(PV_LG + l * 2 + c)),
                   r=[ACC[:, c, :], PV[:, :]], w=[YG[:, c, :]])
            wout_apply(wo, 2, lambda c: YG[:, c, :], t)
        for t in range(NT):
            tile_body(t)

    def rope_tables(s):
        op("sp", lambda e: e.dma_start(out=POSI[:, :], in_=posb[s]), w=[POSI[:, :]], dma=True)
        op("dve", lambda e: e.tensor_copy(ANG[:, :], POSI[:, :]), r=[POSI[:, :]], w=[ANG[:, :]])
        op("dve", lambda e: e.tensor_scalar(ANG[:, :], ANG[:, :], pvc(PV_INV), None, ALU.mult), r=[ANG[:, :], PV[:, :]], w=[ANG[:, :]])
        op("dve", lambda e: e.tensor_scalar(POSI[:, :], ANG[:, :], float(1.0 / (2 * np.pi)), None, ALU.mult),
           r=[ANG[:, :]], w=[POSI[:, :]])
        op("dve", lambda e: e.tensor_copy(KF[:, :], POSI[:, :]), r=[POSI[:, :]], w=[KF[:, :]])
        op("dve", lambda e: e.scalar_tensor_tensor(ANG[:, :], KF[:, :], float(-2 * np.pi), ANG[:, :], ALU.mult, ALU.add),
           r=[KF[:, :], ANG[:, :]], w=[ANG[:, :]])
        op("dve", lambda e: e.tensor_scalar(ANG[:, :], ANG[:, :], 3.1415925, -3.1415925, ALU.min, ALU.max),
           r=[ANG[:, :]], w=[ANG[:, :]])
        op("act", lambda e: e.activation(SN[:, :], ANG[:, :], AF.Sin), r=[ANG[:, :]], w=[SN[:, :]])
        op("dve", lambda e: e.scalar_tensor_tensor(KF[:, :], ANG[:, :], -1.0, ANG[:, :], ALU.mult, ALU.max), r=[ANG[:, :]], w=[KF[:, :]])
        op("act", lambda e: e.activation(CT[:, :], KF[:, :], AF.Sin, bias=pvc(PV_INV + 3), scale=-1.0),
           r=[KF[:, :], PV[:, :]], w=[CT[:, :]])

    def mix_attn(l, kind, i):
        base = OFF_MOBA if kind == "moba" else OFF_DIL
        g = GU[wslot[0] % 4]
        wslot[0] += 1
        g2 = GU[wslot[0] % 4]
        wslot[0] += 1
        gv = load_win(l, g, 0, base + i * 128, 128)
        load_win(l, g, 128, base + 256 + i * 128, 128)
        gvv = load_win(l, g2, 0, base + 512 + i * 128, 128)
        wo = WO[woslot[0] % 2]
        woslot[0] += 1
        row0 = (256 if kind == "moba" else 512) + i * 128
        load_wout(l, row0, 1, wo)
        op("dve", lambda e: e.memset(VP[:, :, :, :], 0.0), w=[VP[:, :, :, :]])

        def proj_tile(t):
            for qk in range(2):
                ps = bank()

                def mm(e, ps=ps, qk=qk):
                    for k in range(KC):
                        ins = e.matmul(ps[:, :], gv[:, k, qk * 128:(qk + 1) * 128], XN[:, k, tl(t)], start=(k == 0), stop=(k == KC - 1))
                    return ins
                op("pe", mm, r=[gv[:, :, 0:256], XN[:, :, tl(t)]], w=[ps[:, :]])
                qb = ET[qk]
                op("act", lambda e, ps=ps, qb=qb: e.activation(qb[:, :], ps[:, :], AF.Copy), r=[ps[:, :]], w=[qb[:, :], ps[:, :]])
                if DBG.get("sub", 9) < 2:
                    continue
                pr = bank()
                op("pe", lambda e, pr=pr, qb=qb: e.matmul(pr[:, :], ROT[:, :], qb[:, :], start=True, stop=True),
                   r=[ROT[:, :], qb[:, :]], w=[pr[:, :]])
                ta, tb = TMP[2 * qk], TMP[2 * qk + 1]
                if DBG.get("sub2", 9) < 1:
                    continue
                op("dve", lambda e, ps=ps, ta=ta: e.tensor_tensor(ta[:, :], ps[:, :], CT[:, tl(t)], ALU.mult),
                   r=[ps[:, :], CT[:, tl(t)]], w=[ta[:, :]])
                if DBG.get("sub2", 9) < 2:
                    continue
                op("dve", lambda e, pr=pr, tb=tb: e.tensor_tensor(tb[:, :], pr[:, :], SN[:, tl(t)], ALU.mult),
                   r=[pr[:, :], SN[:, tl(t)]], w=[tb[:, :]])
                if DBG.get("sub2", 9) < 3:
                    continue
                op("dve", lambda e, ta=ta, tb=tb, qk=qk: e.tensor_tensor(QK[:, qk, tl(t)], ta[:, :], tb[:, :], ALU.add),
                   r=[ta[:, :], tb[:, :]], w=[QK[:, qk, tl(t)]])
            if DBG.get("sub", 9) < 3:
                return
            for j in range(4):
                tc_ = t * 4 + j
                ps = bank()

                def mmv(e, ps=ps, tc_=tc_):
                    for k in range(KC):
                        ins = e.matmul(ps[:, 0:128], XN[:, k, tc_ * 128:(tc_ + 1) * 128], gvv[:, k, 0:128], start=(k == 0), stop=(k == KC - 1))
                    return ins
                op("pe", mmv, r=[gvv[:, :, 0:128], XN[:, :, tc_ * 128:(tc_ + 1) * 128]], w=[ps[:, :]])
                op("act", lambda e, ps=ps, tc_=tc_: e.activation(VP[:, tc_, 0, 0:64], ps[:, 0:64], AF.Copy),
                   r=[ps[:, :]], w=[VP[:, tc_, 0, :]])
                op("act", lambda e, ps=ps, tc_=tc_: e.activation(VP[:, tc_, 1, 64:128], ps[:, 64:128], AF.Copy),
                   r=[ps[:, :]], w=[VP[:, tc_, 1, :]])
        if DBG["attn_level"] < 1:
            return
        for t in range(NT):
            proj_tile(t)
        if DBG["attn_level"] < 2:
            return
        if kind == "moba":
            op("dve", lambda e: e.tensor_reduce(KM[:, :], QK[:, 1, :].rearrange("p (a b) -> p a b", a=8), AX.X, ALU.add),
               r=[QK[:, 1, :]], w=[KM[:, :]])
            op("dve", lambda e: e.tensor_scalar(KMB[:, :], KM[:, :], 1.0 / 256, None, ALU.mult), r=[KM[:, :]], w=[KMB[:, :]])
            pgate = bank()

            def mmgate(e):
                for h in range(2):
                    for qc in range(16):
                        c0 = (h * 16 + qc) * 8
                        ins = e.matmul(pgate[:, c0:c0 + 8], QK[h * 64:(h + 1) * 64, 0, qc * 128:(qc + 1) * 128],
                                       KMB[h * 64:(h + 1) * 64, :], start=True, stop=True)
                return ins
            op("pe", mmgate, r=[QK[:, 0, :], KMB[:, :]], w=[pgate[:, :]])
            for h in range(2):
                op("dve", lambda e, h=h: e.tensor_tensor(
                    GM[:, h, :, :], pgate[:, h * 128:(h + 1) * 128].rearrange("p (a b) -> p a b", a=16), NEGM[:, :, :], ALU.add),
                   r=[pgate[:, :], NEGM[:, :, :]], w=[GM[:, h, :, :]])
            for h in range(2):
                for qc in range(16):
                    op("dve", lambda e, h=h, qc=qc: e.max(MX8[:, h, qc, :], GM[:, h, qc, :]),
                       r=[GM[:, h, qc, :]], w=[MX8[:, h, qc, :]])
            for h in range(2):
                for qc in range(16):
                    op("dve", lambda e, h=h, qc=qc: e.tensor_scalar(BI[:, h, qc, :], GM[:, h, qc, :], MX8[:, h, qc, 2:3], NEG,
                                                                    ALU.is_lt, ALU.mult),
                       r=[GM[:, h, qc, :], MX8[:, h, qc, :]], w=[BI[:, h, qc, :]])
            for h in range(2):
                for q4 in range(4):
                    pt = bank()

                    def tr(e, pt=pt, h=h, q4=q4):
                        for j in range(4):
                            ins = e.transpose(pt[0:8, j * 128:(j + 1) * 128], BI[:, h, q4 * 4 + j, :], IDF[:, :])
                        return ins
                    op("pe", tr, r=[BI[:, h, :, :], IDF[:, :]], w=[pt[:, :]])
                    op("act", lambda e, pt=pt, h=h, q4=q4: e.activation(BIAST[0:8, h, q4 * 512:(q4 + 1) * 512], pt[0:8, :], AF.Copy),
                       r=[pt[:, :]], w=[BIAST[:, h, q4 * 512:(q4 + 1) * 512]])
        if DBG["attn_level"] < 3:
            return
        ecnt = [0]

        def sweep_tile(t):
            pnum = banks[6]
            pden = banks[7]
            nck = 4 * t + 4
            first = True
            for c in range(nck):
                k0 = c * 128
                q0 = max(k0, t * TT)
                q1 = (t + 1) * TT
                n = c // 2
                for h in range(2):
                    hs = slice(h * 64, (h + 1) * 64)
                    pS = bank()
                    Et = ET[ecnt[0] % 4]
                    ecnt[0] += 1
                    w_ = q1 - q0

                    def mms(e, pS=pS, hs=hs, h=h, k0=k0, q0=q0, q1=q1, n=n, w_=w_):
                        extra = []
                        if q0 == k0:
                            extra.append(("tri", 0, 128))
                        if kind == "moba":
                            ob = max(q0, 256 * (n + 1))
                            if ob < q1:
                                extra.append(("sel", ob - q0, q1 - q0, ob, q1))
                        ins = e.matmul(pS[:, 0:w_], QK[hs, 1, k0:k0 + 128], QK[hs, 0, q0:q1], start=True, stop=(len(extra) == 0))
                        for xi, x in enumerate(extra):
                            lastx = (xi == len(extra) - 1)
                            if x[0] == "tri":
                                ins = e.matmul(pS[:, 0:128], IDB[:, :], TRI[:, :], start=False, stop=lastx)
                            else:
                                ins = e.matmul(pS[:, x[1]:x[2]], ESEL[0:8, n, :], BIAST[0:8, h, x[3]:x[4]], start=False, stop=lastx)
                        return ins
                    rr = [QK[:, :, k0:k0 + 128], QK[:, 0, q0:q1], IDB[:, :], TRI[:, :], ESEL[0:8, :, :]]
                    if kind == "moba":
                        rr.append(BIAST[:, h, q0:q1])
                    op("pe", mms, r=rr, w=[pS[:, :]])
                    op("act", lambda e, pS=pS, Et=Et, w_=w_: e.activation(Et[:, 0:w_], pS[:, 0:w_], AF.Exp, scale=0.125),
                       r=[pS[:, :]], w=[Et[:, :]])
                    if kind == "dil":
                        dlt = q0 - k0
                        op("dve", lambda e, Et=Et, w_=w_, dlt=dlt: e.tensor_tensor(Et[:, 0:w_], Et[:, 0:w_], DM[:, dlt:dlt + w_], ALU.mult),
                           r=[Et[:, :], DM[:, dlt:dlt + w_]], w=[Et[:, :]])
                    lastpv = (c == nck - 1 and h == 1)

                    def mmpv(e, Et=Et, w_=w_, q0=q0, c=c, h=h, first=first, lastpv=lastpv):
                        o0 = q0 - t * TT
                        e.matmul(pnum[:, o0:o0 + w_], VP[:, c, h, :], Et[:, 0:w_], start=first, stop=lastpv)
                        return e.matmul(pden[:, o0:o0 + w_], ONEH[:, h, :], Et[:, 0:w_], start=first, stop=lastpv)
                    op("pe", mmpv, r=[VP[:, c, h, :], Et[:, :], ONEH[:, :, :]], w=[pnum[:, :], pden[:, :]])
                    first = False
            rd = TMP[t % 2]
            op("dve", lambda e, rd=rd, pden=pden: e.reciprocal(rd[:, :], pden[:, :]), r=[pden[:, :]], w=[rd[:, :]])
            op("dve", lambda e, rd=rd, pnum=pnum: e.tensor_tensor(YA[:, tl(t)], pnum[:, :], rd[:, :], ALU.mult),
               r=[pnum[:, :], rd[:, :]], w=[YA[:, tl(t)]])
            wout_apply(wo, 1, lambda c: YA[:, tl(t)], t)
        for t in range(NT):
            sweep_tile(t)

    rot_d = din("rot", [128, 128])

    def final_tile(s, t):
        if final_norm:
            ps = bank()
            for c in range(KC):
                sq = SQ[c % 2]
                op("act", lambda e, c=c, sq=sq: e.activation(sq[:, :], X[:, c, tl(t)], AF.Square),
                   r=[X[:, c, tl(t)]], w=[sq[:, :]])
                op("pe", lambda e, c=c, sq=sq: e.matmul(ps[:, :], ONES[:, :], sq[:, :], start=(c == 0), stop=(c == KC - 1)),
                   r=[sq[:, :], ONES[:, :]], w=[ps[:, :]])
            op("act", lambda e: e.activation(RSTD[:, :], ps[:, :], AF.Sqrt, bias=pvc(PV_EPS1), scale=1.0 / D),
               r=[ps[:, :], PV[:, :]], w=[RSTD[:, :]])
            op("dve", lambda e: e.reciprocal(RSTD[:, :], RSTD[:, :]), r=[RSTD[:, :]], w=[RSTD[:, :]])
        for c in range(KC):
            ot = TMP[c % 4]
            if final_norm:
                op("dve", lambda e, c=c, ot=ot: e.scalar_tensor_tensor(ot[:, :], X[:, c, tl(t)], pvc(PV_NF + c), RSTD[:, :],
                                                                      ALU.mult, ALU.mult),
                   r=[X[:, c, tl(t)], RSTD[:, :], PV[:, :]], w=[ot[:, :]])
                src = ot[:, :]
            else:
                src = X[:, c, tl(t)]
            op("sp", lambda e, c=c, src=src: e.dma_start(out=outT[s, c * 128:(c + 1) * 128, tl(t)], in_=src),
               r=[src], dma=True, out_dma=True)

    for s in range(nseq):
        for c in range(KC):
            op("sp", lambda e, c=c, s=s: e.dma_start(out=X[:, c, :], in_=xT[s, c * 128:(c + 1) * 128, :]),
               w=[X[:, c, :]], dma=True)
        for l in range(nlayers):
            if "ffn1" in stages:
                ffn(l, 0)
            if "mix" in stages:
                for t in range(NT):
                    rmsnorm_tile(PV_NM + l * 8, t, XN[:, :, tl(t)])
                if "pool" in mix_parts:
                    mix_pool(l)
                if "conv" in mix_parts:
                    mix_conv(l)
                if "moba" in mix_parts or "dil" in mix_parts:
                    rope_tables(s)
                    op("pool", lambda e: e.dma_start(out=ROT[:, :], in_=rot_d), w=[ROT[:, :]], dma=True)
                if "moba" in mix_parts:
                    for i in range(2):
                        mix_attn(l, "moba", i)
                if "dil" in mix_parts:
                    op("pool", lambda e: e.dma_start(out=DM[:, :], in_=dmask_d[0:128, :]), w=[DM[:, :]], dma=True)
                    for i in range(2):
                        mix_attn(l, "dil", i)
            if "ffn2" in stages:
                ffn(l, 1)
        for t in range(NT):
            final_tile(s, t)

    sc.emit(nc, es)
    es.close()
    return nc, len(sc.ops)


def _consts():
    ident = np.eye(128, dtype=np.float32)
    i = np.arange(128)[:, None]
    j = np.arange(128)[None, :]
    tri = np.where(i <= j, 0.0, NEG).astype(np.float32)
    x = np.arange(S)[None, :]
    dd = x - i
    mult = ((dd >= 0) & (dd <= 128)).astype(np.float32) \
        + ((dd >= 0) & (dd <= 512) & (dd % 4 == 0)).astype(np.float32) \
        + ((dd >= 0) & (dd <= 2048) & (dd % 16 == 0)).astype(np.float32)
    negm = np.zeros((128, 16, 8), np.float32)
    for qc in range(16):
        negm[:, qc, qc // 2:] = -1e30
    esel = np.zeros((8, 8, 128), np.float32)
    for n in range(8):
        esel[n, n, :] = 1.0
    oneh = np.zeros((128, 2, 128), np.float32)
    oneh[:, 0, 0:64] = 1.0
    oneh[:, 1, 64:128] = 1.0
    icnt = np.zeros((128, 2, 16), np.float32)
    wins = {(0, 0): 2, (0, 1): 4, (1, 0): 8, (1, 1): 16}
    for (c, h), w in wins.items():
        icnt[h * 64:(h + 1) * 64, c, :] = 1.0 / np.minimum(np.arange(16) + 1.0, float(w))
    rot = np.zeros((128, 128), np.float32)
    for hb in (0, 64):
        for d in range(8):
            rot[hb + d + 8, hb + d] = -1.0
            rot[hb + d, hb + d + 8] = 1.0
    return dict(ident=ident, tri=tri, dmask=mult.astype(np.float32), negm=negm.reshape(128, 128),
                esel=esel.reshape(8, 1024), oneh=oneh.reshape(128, 256), icnt=icnt.reshape(128, 32), rot=rot)


def _pvec(inp):
    pv = np.zeros((128, PV_NCOL), np.float32)

    def fm(v):
        return np.ascontiguousarray(np.asarray(v, np.float32).reshape(-1, 128).T)
    for l in range(NL):
        pv[:, PV_N1 + l * 8:PV_N1 + l * 8 + 8] = fm(inp["ffn1_norm"][l])
        pv[:, PV_NM + l * 8:PV_NM + l * 8 + 8] = fm(inp["mix_norm"][l])
        pv[:, PV_N2 + l * 8:PV_N2 + l * 8 + 8] = fm(inp["ffn2_norm"][l])
        pv[:, PV_PS + l * 2:PV_PS + l * 2 + 2] = fm(inp["pool_scale"][l])
        cw = np.asarray(inp["conv_w"][l], np.float32)
        for c in range(2):
            pv[:, PV_CW + (l * 2 + c) * 31:PV_CW + (l * 2 + c + 1) * 31] = cw[:, c * 128:(c + 1) * 128].T
        pv[:, PV_CB + l * 2:PV_CB + l * 2 + 2] = fm(inp["conv_b"][l])
        pv[:, PV_LG + l * 2:PV_LG + l * 2 + 2] = fm(inp["conv_ln_g"][l])
        pv[:, PV_LB + l * 2:PV_LB + l * 2 + 2] = fm(inp["conv_ln_b"][l])
    pv[:, PV_NF:PV_NF + 8] = fm(inp["final_norm"])
    inv = (np.float32(500000.0) ** (-np.arange(0, 16, 2, dtype=np.float32) / np.float32(16))).astype(np.float32)
    p = np.arange(128) % 64
    pv[:, PV_INV] = np.where(p < 16, inv[p % 8], 0.0)
    pv[:, PV_INV + 3] = np.float32(np.pi / 2)
    pv[:, PV_EPS1] = 1e-6
    pv[:, PV_EPS2] = 1e-5
    pv[0:64, PV_PIW + 0] = 1.0 / 2
    pv[64:128, PV_PIW + 0] = 1.0 / 4
    pv[0:64, PV_PIW + 1] = 1.0 / 8
    pv[64:128, PV_PIW + 1] = 1.0 / 16
    return pv


_CACHE = {}
SEQ_PER_LAUNCH = 1
CORES_PER_LAUNCH = 1


def kernel(**inputs):
    inp = {k: np.asarray(v) for k, v in inputs.items()}
    x = inp["x"].astype(np.float32, copy=False)
    B = x.shape[0]
    assert B == NCORES * NSEQ
    if "nc" not in _CACHE:
        _CACHE["nc"] = build_program(nseq=SEQ_PER_LAUNCH)[0]
    nc = _CACHE["nc"]
    cst = _consts()
    pv = _pvec(inp)
    shared = {k: np.ascontiguousarray(inp[k], dtype=np.float32) for k in
              ("ffn1_gate", "ffn1_up", "ffn1_down", "ffn2_gate", "ffn2_up", "ffn2_down", "w_in", "w_out", "pool_w")}
    shared.update(cst)
    shared["pvec"] = pv
    out = np.empty((B, S, D), np.float32)
    for s0 in range(0, NSEQ, SEQ_PER_LAUNCH):
        in_maps = []
        for c in range(NCORES):
            b0 = c * NSEQ + s0
            xs = x[b0:b0 + SEQ_PER_LAUNCH]
            m = dict(shared)
            m["xT"] = np.ascontiguousarray(xs.transpose(0, 2, 1))
            ps = inp["positions"][b0:b0 + SEQ_PER_LAUNCH].astype(np.int32)
            m["posb"] = np.ascontiguousarray(np.broadcast_to(ps[:, None, :], (SEQ_PER_LAUNCH, 128, S)))
            in_maps.append(m)
        for g0 in range(0, NCORES, CORES_PER_LAUNCH):
            res = run_bass_kernel_spmd(nc, in_maps[g0:g0 + CORES_PER_LAUNCH], core_ids=list(range(CORES_PER_LAUNCH)))
            for j in range(CORES_PER_LAUNCH):
                c = g0 + j
                o = res.results[j]["outT"]
                b0 = c * NSEQ + s0
                out[b0:b0 + SEQ_PER_LAUNCH] = o.transpose(0, 2, 1)
    return out
```

```python
import numpy as np
from contextlib import ExitStack
import concourse.bass as bass
import concourse.mybir as mybir
from concourse.bass_utils import run_bass_kernel_spmd

F32 = mybir.dt.float32
BF16 = mybir.dt.bfloat16
I32 = mybir.dt.int32
ALU = mybir.AluOpType
AF = mybir.ActivationFunctionType
AX = mybir.AxisListType

D = 1024
S = 2048
FF = 2816
NL = 2
KC = 8
FC = 22
TT = 512
NT = 4
NSEQ = 2
NCORES = 8
D_IN = 2304
OFF_POOL, OFF_MOBA, OFF_DIL, OFF_CONV = 0, 256, 1024, 1792
NEG = -30000.0
GRAN = 1024
DBG = {"attn_level": 9}

PV_N1, PV_NM, PV_N2, PV_NF = 0, 16, 32, 48
PV_PS = 56
PV_CW = 60
PV_CB, PV_LG, PV_LB = 184, 188, 192
PV_INV = 196
PV_PIW = 197
PV_EPS1, PV_EPS2 = 200, 201
PV_NCOL = 204


class Sched:
    ENG = ("pe", "act", "dve", "pool", "sp")
    EPOCH = 1000

    def __init__(self):
        self.ops = []
        self.last_w = {}
        self.readers = {}

    @staticmethod
    def keys(ap):
        name = ap.name
        pat = ap.ap
        es = mybir.dt.size(ap.dtype)
        pstride = pat[0][0]
        off = ap.offset % pstride if pstride > 0 else ap.offset
        dims = [d for d in pat[1:] if d[1] > 1]
        if not dims:
            dims = [(1, 1)]
        last = dims[-1]
        run = (last[1] - 1) * abs(last[0]) + 1
        outer = dims[:-1]
        ks = set()
        idx = [0] * len(outer)
        while True:
            o = off + sum(i * d[0] for i, d in zip(idx, outer))
            lo = o * es
            hi = (o + run) * es - 1
            for g in range(lo // GRAN, hi // GRAN + 1):
                ks.add((name, g))
            j = len(outer) - 1
            while j >= 0:
                idx[j] += 1
                if idx[j] < outer[j][1]:
                    break
                idx[j] = 0
                j -= 1
            if j < 0:
                break
        return ks

    def op(self, eng, fn, r=(), w=(), dma=False, out_dma=False):
        deps = set()
        rk = set()
        wk = set()
        for a in r:
            rk |= self.keys(a)
        for a in w:
            wk |= self.keys(a)
        for k in rk:
            if k in self.last_w:
                deps.add(self.last_w[k])
        for k in wk:
            if k in self.last_w:
                deps.add(self.last_w[k])
            rd = self.readers.get(k)
            if rd:
                deps.update(rd.values())
        i = len(self.ops)
        self.ops.append(dict(eng=eng, fn=fn, deps=deps, dma=dma, out=out_dma))
        tag = ("dma", i) if dma else eng
        for k in rk:
            self.readers.setdefault(k, {})[tag] = i
        for k in wk:
            self.last_w[k] = i
            self.readers[k] = {}
        return i

    def emit(self, nc, es):
        ops = self.ops
        signal = [False] * len(ops)
        for o in ops:
            for d in o["deps"]:
                if not ops[d]["dma"]:
                    if not (ops[d]["eng"] == "pe" and o["eng"] == "pe" and not o["dma"]):
                        signal[d] = True
        cnt = {e: 0 for e in self.ENG}
        sigcnt = [0] * len(ops)
        for i, o in enumerate(ops):
            if signal[i]:
                cnt[o["eng"]] += 1
                sigcnt[i] = cnt[o["eng"]]
        sems = {}
        for e in self.ENG:
            n = cnt[e] // self.EPOCH + 1
            sems[e] = [es.enter_context(nc.semaphore("s_%s_%d" % (e, j))) for j in range(n)]
        NDS = 12
        dsem = {e: [es.enter_context(nc.semaphore("d_%s_%d" % (e, j))) for j in range(NDS)]
                for e in ("sp", "pool", "act")}
        dcount = {e: [0] * NDS for e in dsem}
        dnext = {e: 0 for e in dsem}
        dma_tok = {}
        out_dmas = []
        per_eng = {e: [] for e in self.ENG}
        for i, o in enumerate(ops):
            per_eng[o["eng"]].append(i)
        for i, o in enumerate(ops):
            if o["dma"]:
                e = o["eng"]
                k = dnext[e]
                dnext[e] = (k + 1) % NDS
                prev = dcount[e][k]
                dcount[e][k] += 1
                dma_tok[i] = (dsem[e][k], 16 * (prev + 1), 16 * prev)
                if o["out"]:
                    out_dmas.append(i)
        EP = self.EPOCH
        if DBG.get('verbose'):
            print('signals', cnt, 'dma per sem', {e: max(v) for e, v in dcount.items()}, flush=True)

        def run_engine(ename, e):
            waited = {f: 0 for f in self.ENG}
            dwaited = {}
            for i in per_eng[ename]:
                o = ops[i]
                need = {}
                for d in o["deps"]:
                    od = ops[d]
                    if od["dma"]:
                        sem, val, _ = dma_tok[d]
                        if dwaited.get(sem.num if hasattr(sem, "num") else id(sem), 0) < val:
                            dwaited[sem.num if hasattr(sem, "num") else id(sem)] = val
                            e.wait_ge(sem, val)
                    else:
                        f = od["eng"]
                        if f == "pe" and ename == "pe" and not o["dma"]:
                            continue
                        c = sigcnt[d]
                        if c > need.get(f, 0):
                            need[f] = c
                for f, c in need.items():
                    if c > waited[f]:
                        waited[f] = c
                        e.wait_ge(sems[f][(c - 1) // EP], (c - 1) % EP + 1)
                if o["dma"]:
                    sem, val, prev = dma_tok[i]
                    key = sem.num if hasattr(sem, "num") else id(sem)
                    if prev > 0 and dwaited.get(key, 0) < prev:
                        dwaited[key] = prev
                        e.wait_ge(sem, prev)
                    ins = o["fn"](e)
                    ins.then_inc(sem, 16)
                else:
                    ins = o["fn"](e)
                    if signal[i]:
                        c = sigcnt[i]
                        ins.then_inc(sems[ename][(c - 1) // EP], 1)
            if ename == "sp":
                for i in out_dmas:
                    sem, val, _ = dma_tok[i]
                    e.wait_ge(sem, val)

        block = es.enter_context(nc.Block())

        @block.tensor
        def _(e):
            run_engine("pe", e)

        @block.scalar
        def _(e):
            run_engine("act", e)

        @block.vector
        def _(e):
            run_engine("dve", e)

        @block.gpsimd
        def _(e):
            run_engine("pool", e)

        @block.sync
        def _(e):
            run_engine("sp", e)


def build_program(nseq=NSEQ, nlayers=NL, stages=("ffn1", "mix", "ffn2"), final_norm=True,
                  mix_parts=("pool", "conv", "moba", "dil")):
    nc = bass.Bass("TRN2", target_bir_lowering=False)
    es = ExitStack()

    def din(name, shape, dt=F32):
        return nc.dram_tensor(name, list(shape), dt, kind="ExternalInput").ap()

    xT = din("xT", [nseq, D, S])
    posb = din("posb", [nseq, 128, S], I32)
    wg = [din("ffn1_gate", [NL, D, FF]), din("ffn2_gate", [NL, D, FF])]
    wu = [din("ffn1_up", [NL, D, FF]), din("ffn2_up", [NL, D, FF])]
    wd = [din("ffn1_down", [NL, FF, D]), din("ffn2_down", [NL, FF, D])]
    w_in = din("w_in", [NL, D, D_IN])
    w_out = din("w_out", [NL, D, D])
    pool_w = din("pool_w", [NL, 4, 64, 64])
    pvec_d = din("pvec", [128, PV_NCOL])
    ident_d = din("ident", [128, 128])
    tri_d = din("tri", [128, 128])
    dmask_d = din("dmask", [128, S])
    negm_d = din("negm", [128, 128])
    esel_d = din("esel", [8, 8 * 128])
    oneh_d = din("oneh", [128, 256])
    icnt_d = din("icnt", [128, 32])
    outT = nc.dram_tensor("outT", [nseq, D, S], F32, kind="ExternalOutput").ap()

    ARENA_BYTES = 207 * 1024
    A = es.enter_context(nc.sbuf_tensor("A", [128, ARENA_BYTES // 4], F32))
    banks = [es.enter_context(nc.psum_tensor("P%d" % i, [128, 512], F32)) for i in range(8)]

    cur = [0]

    def alloc(shape, dt, at=None):
        esz = mybir.dt.size(dt)
        n = int(np.prod(shape)) * esz
        n_al = (n + GRAN - 1) // GRAN * GRAN
        if at is None:
            off = cur[0]
            cur[0] += n_al
        else:
            off = at
        assert off + n <= ARENA_BYTES, ("arena overflow", off, n)
        a = A[:, off // 4:(off + n) // 4]
        if dt != F32:
            a = a.bitcast(dt)
        if len(shape) == 2:
            a = a.rearrange("p (a b) -> p a b", a=shape[0], b=shape[1])
        elif len(shape) == 3:
            a = a.rearrange("p (a b c) -> p a b c", a=shape[0], b=shape[1], c=shape[2])
        return a

    KB = 1024
    X = alloc([KC, S], F32)
    XN = alloc([KC, S], BF16)
    PV = alloc([PV_NCOL], F32)
    IDB = alloc([128], BF16)
    IDF = alloc([128], F32)
    ONES = alloc([128], BF16)
    ONEH = alloc([2, 128], BF16)
    TRI = alloc([128], BF16)
    NEGM = alloc([16, 8], F32)
    ESEL = alloc([8, 128], BF16)
    ICNT = alloc([2, 16], F32)
    GU = [alloc([2, KC, 128], BF16) for _ in range(4)]
    SQ = [alloc([TT], BF16) for _ in range(2)]
    RSTD = alloc([TT], F32)
    TMP = [alloc([TT], F32) for _ in range(4)]
    ROT = alloc([128], BF16)
    phase_base = cur[0]
    HT = alloc([11, S], BF16)
    WD = [alloc([11, 128], BF16) for _ in range(3)]
    ffn_end = cur[0]
    cur[0] = phase_base
    CT = alloc([S], F32)
    SN = alloc([S], F32)
    QK = alloc([2, S], BF16)
    VP = alloc([16, 2, 128], BF16)
    WO = [alloc([2, D], BF16) for _ in range(2)]
    YG = alloc([2, TT], BF16)
    YA = alloc([S], BF16)
    ET = [alloc([TT], BF16) for _ in range(4)]
    att_base = cur[0]
    DM = alloc([S], BF16)
    BIAST = alloc([2, S], BF16)
    GM = alloc([2, 16, 8], F32)
    MX8 = alloc([2, 16, 8], F32)
    BI = alloc([2, 16, 8], F32)
    KM = alloc([8], F32)
    KMB = alloc([8], BF16)
    att_end = cur[0]
    cur[0] = att_base
    U = alloc([2, 16 + TT], F32)
    T1 = alloc([16 + TT], F32)
    T2 = alloc([16 + TT], F32)
    PB = alloc([TT], BF16)
    FIX = alloc([16], F32)
    PW = alloc([2, 128], BF16)
    pool_end = cur[0]
    cur[0] = att_base
    HC = alloc([2, 30 + TT], F32)
    ACC = alloc([2, TT], F32)
    YB = alloc([2, TT], BF16)
    pc_end = max(cur[0], pool_end)
    cur[0] = att_base
    POSI = alloc([S], I32)
    ANG = alloc([S], F32)
    KF = CT
    rope_end = cur[0]
    assert max(ffn_end, att_end, pc_end, rope_end) <= ARENA_BYTES, (ffn_end, att_end, pc_end, rope_end)

    sc = Sched()
    op = sc.op
    psrr = [0]

    def bank():
        b = banks[psrr[0] % 6]
        psrr[0] += 1
        return b

    def pvc(col, p0=0, p1=128):
        return PV[p0:p1, col:col + 1]

    op("sp", lambda e: e.dma_start(out=PV[:, :], in_=pvec_d), w=[PV[:, :]], dma=True)
    op("sp", lambda e: e.dma_start(out=IDF[:, :], in_=ident_d), w=[IDF[:, :]], dma=True)
    op("sp", lambda e: e.dma_start(out=NEGM[:, :, :], in_=negm_d.rearrange("p (a b) -> p a b", a=16)),
       w=[NEGM[:, :, :]], dma=True)
    op("sp", lambda e: e.dma_start(out=ICNT[:, :, :], in_=icnt_d.rearrange("p (a b) -> p a b", a=2)),
       w=[ICNT[:, :, :]], dma=True)
    op("pool", lambda e: e.dma_start(out=IDB[:, :], in_=ident_d), w=[IDB[:, :]], dma=True)
    op("pool", lambda e: e.dma_start(out=TRI[:, :], in_=tri_d), w=[TRI[:, :]], dma=True)
    op("pool", lambda e: e.dma_start(out=ONEH[:, :, :], in_=oneh_d.rearrange("p (a b) -> p a b", a=2)),
       w=[ONEH[:, :, :]], dma=True)
    op("pool", lambda e: e.dma_start(out=ESEL[0:8, :, :], in_=esel_d.rearrange("p (a b) -> p a b", a=8)),
       w=[ESEL[0:8, :, :]], dma=True)
    op("dve", lambda e: e.memset(ONES[:, :], 1.0), w=[ONES[:, :]])

    def tl(t):
        return slice(t * TT, (t + 1) * TT)

    def rmsnorm_tile(gcol, t, dst):
        ps = bank()
        for c in range(KC):
            sq = SQ[c % 2]
            op("act", lambda e, c=c, sq=sq: e.activation(sq[:, :], X[:, c, tl(t)], AF.Square),
               r=[X[:, c, tl(t)]], w=[sq[:, :]])
            op("pe", lambda e, c=c, sq=sq: e.matmul(ps[:, :], ONES[:, :], sq[:, :], start=(c == 0), stop=(c == KC - 1)),
               r=[sq[:, :], ONES[:, :]], w=[ps[:, :]])
        op("act", lambda e: e.activation(RSTD[:, :], ps[:, :], AF.Sqrt, bias=pvc(PV_EPS1), scale=1.0 / D),
           r=[ps[:, :], PV[:, :]], w=[RSTD[:, :]])
        op("dve", lambda e: e.reciprocal(RSTD[:, :], RSTD[:, :]), r=[RSTD[:, :]], w=[RSTD[:, :]])
        for c in range(KC):
            op("dve", lambda e, c=c: e.scalar_tensor_tensor(dst[:, c, :], X[:, c, tl(t)], pvc(gcol + c), RSTD[:, :],
                                                           ALU.mult, ALU.mult),
               r=[X[:, c, tl(t)], RSTD[:, :], PV[:, :]], w=[dst[:, c, :]])

    def ffn(l, which):
        gcol = (PV_N1 if which == 0 else PV_N2) + l * 8
        Wg, Wu, Wd = wg[which][l], wu[which][l], wd[which][l]
        for t in range(NT):
            rmsnorm_tile(gcol, t, XN[:, :, tl(t)])
        slot = [0]
        dslot = [0]
        for half in range(2):
            for fl in range(11):
                f = half * 11 + fl
                g = GU[slot[0] % 4]
                slot[0] += 1
                for gi, Wsrc in ((0, Wg), (1, Wu)):
                    for kh in range(2):
                        op("pool", lambda e, g=g, f=f, gi=gi, Wsrc=Wsrc, kh=kh: e.dma_start(
                            out=g[:, gi, kh * 4:(kh + 1) * 4, :],
                            in_=Wsrc[kh * 512:(kh + 1) * 512, f * 128:(f + 1) * 128].rearrange("(k p) n -> p k n", p=128)),
                           w=[g[:, gi, kh * 4:(kh + 1) * 4, :]], dma=True)
                for t in range(NT):
                    pg = bank()
                    pu = bank()

                    def mmg(e, g=g, t=t, pg=pg, pu=pu):
                        for k in range(KC):
                            e.matmul(pg[:, :], g[:, 0, k, :], XN[:, k, tl(t)], start=(k == 0), stop=(k == KC - 1))
                        for k in range(KC):
                            ins = e.matmul(pu[:, :], g[:, 1, k, :], XN[:, k, tl(t)], start=(k == 0), stop=(k == KC - 1))
                        return ins
                    op("pe", mmg, r=[g[:, :, :, :], XN[:, :, tl(t)]], w=[pg[:, :], pu[:, :]])
                    sg = TMP[(f * NT + t) % 2]
                    op("act", lambda e, sg=sg, pg=pg: e.activation(sg[:, :], pg[:, :], AF.Silu),
                       r=[pg[:, :]], w=[sg[:, :]])
                    op("dve", lambda e, sg=sg, pu=pu, fl=fl, t=t: e.tensor_tensor(HT[:, fl, tl(t)], sg[:, :], pu[:, :], ALU.mult),
                       r=[sg[:, :], pu[:, :]], w=[HT[:, fl, tl(t)]])
            for d in range(KC):
                wdt = WD[dslot[0] % 3]
                dslot[0] += 1
                for f0, f1 in ((0, 4), (4, 8), (8, 11)):
                    op("pool", lambda e, wdt=wdt, d=d, half=half, f0=f0, f1=f1: e.dma_start(
                        out=wdt[:, f0:f1, :],
                        in_=Wd[(half * 11 + f0) * 128:(half * 11 + f1) * 128, d * 128:(d + 1) * 128].rearrange("(f p) n -> p f n", p=128)),
                       w=[wdt[:, f0:f1, :]], dma=True)
                for t in range(NT):
                    pa = bank()

                    def mmd(e, wdt=wdt, t=t, pa=pa):
                        for fl in range(11):
                            ins = e.matmul(pa[:, :], wdt[:, fl, :], HT[:, fl, tl(t)], start=(fl == 0), stop=(fl == 10))
                        return ins
                    op("pe", mmd, r=[wdt[:, :, :], HT[:, :, tl(t)]], w=[pa[:, :]])
                    op("dve", lambda e, pa=pa, d=d, t=t: e.scalar_tensor_tensor(
                        X[:, d, tl(t)], pa[:, :], 0.5, X[:, d, tl(t)], ALU.mult, ALU.add),
                       r=[pa[:, :], X[:, d, tl(t)]], w=[X[:, d, tl(t)]])

    def wout_apply(wo, nch, rhs_of, t):
        for d in range(KC):
            pa = bank()

            def mm(e, pa=pa, d=d):
                for c in range(nch):
                    ins = e.matmul(pa[:, :], wo[:, c, d * 128:(d + 1) * 128], rhs_of(c), start=(c == 0), stop=(c == nch - 1))
                return ins
            op("pe", mm, r=[wo[:, 0:nch, :]] + [rhs_of(c) for c in range(nch)], w=[pa[:, :]])
            op("dve", lambda e, pa=pa, d=d: e.tensor_tensor(X[:, d, tl(t)], pa[:, :], X[:, d, tl(t)], ALU.add),
               r=[pa[:, :], X[:, d, tl(t)]], w=[X[:, d, tl(t)]])

    def load_wout(l, row0, nch, wo):
        op("pool", lambda e: e.dma_start(
            out=wo[:, 0:nch, :], in_=w_out[l][row0:row0 + nch * 128, :].rearrange("(c p) n -> p c n", p=128)),
           w=[wo[:, 0:nch, :]], dma=True)

    def load_win(l, g, dst_off, col0, ncol):
        gv = g.rearrange("p a k n -> p (a k n)").rearrange("p (k n) -> p k n", k=KC)
        for kh in range(2):
            op("pool", lambda e, kh=kh: e.dma_start(
                out=gv[:, kh * 4:(kh + 1) * 4, dst_off:dst_off + ncol],
                in_=w_in[l][kh * 512:(kh + 1) * 512, col0:col0 + ncol].rearrange("(k p) n -> p k n", p=128)),
               w=[gv[:, kh * 4:(kh + 1) * 4, dst_off:dst_off + ncol]], dma=True)
        return gv

    wslot = [0]
    woslot = [0]

    def mix_pool(l):
        g = GU[wslot[0] % 4]
        wslot[0] += 1
        gv = load_win(l, g, 0, OFF_POOL, 256)
        wo = WO[woslot[0] % 2]
        woslot[0] += 1
        load_wout(l, 0, 2, wo)
        op("dve", lambda e: e.memset(PW[:, :, :], 0.0), w=[PW[:, :, :]])
        for c in range(2):
            for h in range(2):
                op("pool", lambda e, c=c, h=h: e.dma_start(out=PW[h * 64:(h + 1) * 64, c, h * 64:(h + 1) * 64],
                                                           in_=pool_w[l, 2 * c + h]),
                   w=[PW[:, c, :]], dma=True)
        op("dve", lambda e: e.memset(U[:, :, 0:16], 0.0), w=[U[:, :, 0:16]])
        W = 16 + TT

        def level(dst, src, sh, lo):
            op("dve", lambda e: e.tensor_tensor(dst[:, lo:W], src[:, lo:W], src[:, lo - sh:W - sh], ALU.add),
               r=[src[:, 0:W]], w=[dst[:, 0:W]])

        def finish(sw, c, p0, p1, t):
            op("dve", lambda e: e.scalar_tensor_tensor(
                PB[p0:p1, :], sw[p0:p1, 16:W], pvc(PV_PIW + c, p0, p1), U[p0:p1, c, 16:W], ALU.mult, ALU.subtract),
               r=[sw[:, 0:W], U[:, c, 16:W], PV[:, :]], w=[PB[:, :]])
            if t == 0:
                op("dve", lambda e: e.tensor_tensor(FIX[p0:p1, :], sw[p0:p1, 16:32], ICNT[p0:p1, c, :], ALU.mult),
                   r=[sw[:, 0:W], ICNT[:, :, :]], w=[FIX[:, :]])
                op("dve", lambda e: e.tensor_tensor(PB[p0:p1, 0:16], FIX[p0:p1, :], U[p0:p1, c, 16:32], ALU.subtract),
                   r=[FIX[:, :], U[:, c, 16:32]], w=[PB[:, :]])

        def tile_body(t):
            for c in range(2):
                ps = bank()

                def mm(e, ps=ps, c=c):
                    for k in range(KC):
                        ins = e.matmul(ps[:, :], gv[:, k, c * 128:(c + 1) * 128], XN[:, k, tl(t)], start=(k == 0), stop=(k == KC - 1))
                    return ins
                op("pe", mm, r=[gv[:, :, 0:256], XN[:, :, tl(t)]], w=[ps[:, :]])
                if t > 0:
                    op("dve", lambda e, c=c: e.tensor_copy(U[:, c, 0:16], U[:, c, TT:TT + 16]),
                       r=[U[:, c, TT:TT + 16]], w=[U[:, c, 0:16]])
                op("act", lambda e, ps=ps, c=c: e.activation(U[:, c, 16:W], ps[:, :], AF.Copy),
                   r=[ps[:, :]], w=[U[:, c, 16:W]])
                Uc = U[:, c, :]
                level(T1, Uc, 1, 1)
                if c == 0:
                    finish(T1, c, 0, 64, t)
                    level(T2, T1, 2, 3)
                    finish(T2, c, 64, 128, t)
                else:
                    level(T2, T1, 2, 3)
                    level(T1, T2, 4, 7)
                    finish(T1, c, 0, 64, t)
                    level(T2, T1, 8, 15)
                    finish(T2, c, 64, 128, t)
                ps2 = bank()
                op("pe", lambda e, ps2=ps2, c=c: e.matmul(ps2[:, :], PW[:, c, :], PB[:, :], start=True, stop=True),
                   r=[PW[:, c, :], PB[:, :]], w=[ps2[:, :]])
                op("act", lambda e, ps2=ps2, c=c: e.activation(YG[:, c, :], ps2[:, :], AF.Copy, scale=pvc(PV_PS + l * 2 + c)),
                   r=[ps2[:, :], PV[:, :]], w=[YG[:, c, :]])
            wout_apply(wo, 2, lambda c: YG[:, c, :], t)
        for t in range(NT):
            tile_body(t)

    def mix_conv(l):
        g = GU[wslot[0] % 4]
        wslot[0] += 1
        g2 = GU[wslot[0] % 4]
        wslot[0] += 1
        gva = load_win(l, g, 0, OFF_CONV, 256)
        gvg = load_win(l, g2, 0, OFF_CONV + 256, 256)
        wo = WO[woslot[0] % 2]
        woslot[0] += 1
        load_wout(l, 768, 2, wo)
        op("dve", lambda e: e.memset(HC[:, :, 0:30], 0.0), w=[HC[:, :, 0:30]])

        def tile_body(t):
            for c in range(2):
                pa = bank()
                pg = bank()

                def mm(e, pa=pa, pg=pg, c=c):
                    for k in range(KC):
                        e.matmul(pa[:, :], gva[:, k, c * 128:(c + 1) * 128], XN[:, k, tl(t)], start=(k == 0), stop=(k == KC - 1))
                    for k in range(KC):
                        ins = e.matmul(pg[:, :], gvg[:, k, c * 128:(c + 1) * 128], XN[:, k, tl(t)], start=(k == 0), stop=(k == KC - 1))
                    return ins
                op("pe", mm, r=[gva[:, :, 0:256], gvg[:, :, 0:256], XN[:, :, tl(t)]], w=[pa[:, :], pg[:, :]])
                sg = TMP[c]
                op("act", lambda e, sg=sg, pg=pg: e.activation(sg[:, :], pg[:, :], AF.Sigmoid), r=[pg[:, :]], w=[sg[:, :]])
                if t > 0:
                    op("dve", lambda e, c=c: e.tensor_copy(HC[:, c, 0:30], HC[:, c, TT:TT + 30]),
                       r=[HC[:, c, TT:TT + 30]], w=[HC[:, c, 0:30]])
                op("dve", lambda e, sg=sg, pa=pa, c=c: e.tensor_tensor(HC[:, c, 30:30 + TT], pa[:, :], sg[:, :], ALU.mult),
                   r=[pa[:, :], sg[:, :]], w=[HC[:, c, 30:30 + TT]])
                cwb = PV_CW + (l * 2 + c) * 31
                op("dve", lambda e, c=c, cwb=cwb: e.tensor_scalar(ACC[:, c, :], HC[:, c, 0:TT], pvc(cwb), pvc(PV_CB + l * 2 + c),
                                                                  ALU.mult, ALU.add),
                   r=[HC[:, c, :], PV[:, :]], w=[ACC[:, c, :]])
                for j in range(1, 31):
                    op("dve", lambda e, c=c, cwb=cwb, j=j: e.scalar_tensor_tensor(
                        ACC[:, c, :], HC[:, c, j:j + TT], pvc(cwb + j), ACC[:, c, :], ALU.mult, ALU.add),
                       r=[HC[:, c, :], ACC[:, c, :], PV[:, :]], w=[ACC[:, c, :]])
            pm = bank()
            pq = bank()
            for c in range(2):
                op("act", lambda e, c=c: e.activation(YB[:, c, :], ACC[:, c, :], AF.Copy), r=[ACC[:, c, :]], w=[YB[:, c, :]])
                op("pe", lambda e, c=c, pm=pm: e.matmul(pm[:, :], ONES[:, :], YB[:, c, :], start=(c == 0), stop=(c == 1)),
                   r=[YB[:, c, :], ONES[:, :]], w=[pm[:, :]])
            for c in range(2):
                sq = SQ[c]
                op("act", lambda e, c=c, sq=sq: e.activation(sq[:, :], ACC[:, c, :], AF.Square), r=[ACC[:, c, :]], w=[sq[:, :]])
                op("pe", lambda e, c=c, sq=sq, pq=pq: e.matmul(pq[:, :], ONES[:, :], sq[:, :], start=(c == 0), stop=(c == 1)),
                   r=[sq[:, :], ONES[:, :]], w=[pq[:, :]])
            MU, VAR = TMP[2], TMP[3]
            op("dve", lambda e, pm=pm: e.tensor_scalar(MU[:, :], pm[:, :], 1.0 / 256, None, ALU.mult), r=[pm[:, :]], w=[MU[:, :]])
            op("dve", lambda e: e.tensor_tensor(VAR[:, :], MU[:, :], MU[:, :], ALU.mult), r=[MU[:, :]], w=[VAR[:, :]])
            op("dve", lambda e, pq=pq: e.scalar_tensor_tensor(VAR[:, :], pq[:, :], 1.0 / 256, VAR[:, :], ALU.mult, ALU.subtract),
               r=[pq[:, :], VAR[:, :]], w=[VAR[:, :]])
            op("act", lambda e: e.activation(VAR[:, :], VAR[:, :], AF.Sqrt, bias=pvc(PV_EPS2), scale=1.0), r=[VAR[:, :], PV[:, :]], w=[VAR[:, :]])
            op("dve", lambda e: e.reciprocal(VAR[:, :], VAR[:, :]), r=[VAR[:, :]], w=[VAR[:, :]])
            for c in range(2):
                op("dve", lambda e, c=c: e.tensor_tensor(ACC[:, c, :], ACC[:, c, :], MU[:, :], ALU.subtract),
                   r=[ACC[:, c, :], MU[:, :]], w=[ACC[:, c, :]])
                op("dve", lambda e, c=c: e.tensor_tensor(ACC[:, c, :], ACC[:, c, :], VAR[:, :], ALU.mult),
                   r=[ACC[:, c, :], VAR[:, :]], w=[ACC[:, c, :]])
                op("act", lambda e, c=c: e.activation(YG[:, c, :], ACC[:, c, :], AF.Silu,
                                                      bias=pvc(PV_LB + l * 2 + c), scale=pvc(PV_LG + l * 2 + c)),
                   r=[ACC[:, c, :], PV[:, :]], w=[YG[:, c, :]])
            wout_apply(wo, 2, lambda c: YG[:, c, :], t)
        for t in range(NT):
            tile_body(t)

    def rope_tables(s):
        op("sp", lambda e: e.dma_start(out=POSI[:, :], in_=posb[s]), w=[POSI[:, :]], dma=True)
        op("dve", lambda e: e.tensor_copy(ANG[:, :], POSI[:, :]), r=[POSI[:, :]], w=[ANG[:, :]])
        op("dve", lambda e: e.tensor_scalar(ANG[:, :], ANG[:, :], pvc(PV_INV), None, ALU.mult), r=[ANG[:, :], PV[:, :]], w=[ANG[:, :]])
        op("dve", lambda e: e.tensor_scalar(POSI[:, :], ANG[:, :], float(1.0 / (2 * np.pi)), None, ALU.mult),
           r=[ANG[:, :]], w=[POSI[:, :]])
        op("dve", lambda e: e.tensor_copy(KF[:, :], POSI[:, :]), r=[POSI[:, :]], w=[KF[:, :]])
        op("dve", lambda e: e.scalar_tensor_tensor(ANG[:, :], KF[:, :], float(-2 * np.pi), ANG[:, :], ALU.mult, ALU.add),
           r=[KF[:, :], ANG[:, :]], w=[ANG[:, :]])
        op("dve", lambda e: e.tensor_scalar(ANG[:, :], ANG[:, :], 3.1415925, -3.1415925, ALU.min, ALU.max),
           r=[ANG[:, :]], w=[ANG[:, :]])
        op("act", lambda e: e.activation(SN[:, :], ANG[:, :], AF.Sin), r=[ANG[:, :]], w=[SN[:, :]])
        op("dve", lambda e: e.scalar_tensor_tensor(KF[:, :], ANG[:, :], -1.0, ANG[:, :], ALU.mult, ALU.max), r=[ANG[:, :]], w=[KF[:, :]])
        op("act", lambda e: e.activation(CT[:, :], KF[:, :], AF.Sin, bias=pvc(PV_INV + 3), scale=-1.0),
           r=[KF[:, :], PV[:, :]], w=[CT[:, :]])

    def mix_attn(l, kind, i):
        base = OFF_MOBA if kind == "moba" else OFF_DIL
        g = GU[wslot[0] % 4]
        wslot[0] += 1
        g2 = GU[wslot[0] % 4]
        wslot[0] += 1
        gv = load_win(l, g, 0, base + i * 128, 128)
        load_win(l, g, 128, base + 256 + i * 128, 128)
        gvv = load_win(l, g2, 0, base + 512 + i * 128, 128)
        wo = WO[woslot[0] % 2]
        woslot[0] += 1
        row0 = (256 if kind == "moba" else 512) + i * 128
        load_wout(l, row0, 1, wo)
        op("dve", lambda e: e.memset(VP[:, :, :, :], 0.0), w=[VP[:, :, :, :]])

        def proj_tile(t):
            for qk in range(2):
                ps = bank()

                def mm(e, ps=ps, qk=qk):
                    for k in range(KC):
                        ins = e.matmul(ps[:, :], gv[:, k, qk * 128:(qk + 1) * 128], XN[:, k, tl(t)], start=(k == 0), stop=(k == KC - 1))
                    return ins
                op("pe", mm, r=[gv[:, :, 0:256], XN[:, :, tl(t)]], w=[ps[:, :]])
                qb = ET[qk]
                op("act", lambda e, ps=ps, qb=qb: e.activation(qb[:, :], ps[:, :], AF.Copy), r=[ps[:, :]], w=[qb[:, :], ps[:, :]])
                if DBG.get("sub", 9) < 2:
                    continue
                pr = bank()
                op("pe", lambda e, pr=pr, qb=qb: e.matmul(pr[:, :], ROT[:, :], qb[:, :], start=True, stop=True),
                   r=[ROT[:, :], qb[:, :]], w=[pr[:, :]])
                ta, tb = TMP[2 * qk], TMP[2 * qk + 1]
                if DBG.get("sub2", 9) < 1:
                    continue
                op("dve", lambda e, ps=ps, ta=ta: e.tensor_tensor(ta[:, :], ps[:, :], CT[:, tl(t)], ALU.mult),
                   r=[ps[:, :], CT[:, tl(t)]], w=[ta[:, :]])
                if DBG.get("sub2", 9) < 2:
                    continue
                op("dve", lambda e, pr=pr, tb=tb: e.tensor_tensor(tb[:, :], pr[:, :], SN[:, tl(t)], ALU.mult),
                   r=[pr[:, :], SN[:, tl(t)]], w=[tb[:, :]])
                if DBG.get("sub2", 9) < 3:
                    continue
                op("dve", lambda e, ta=ta, tb=tb, qk=qk: e.tensor_tensor(QK[:, qk, tl(t)], ta[:, :], tb[:, :], ALU.add),
                   r=[ta[:, :], tb[:, :]], w=[QK[:, qk, tl(t)]])
            if DBG.get("sub", 9) < 3:
                return
            for j in range(4):
                tc_ = t * 4 + j
                ps = bank()

                def mmv(e, ps=ps, tc_=tc_):
                    for k in range(KC):
                        ins = e.matmul(ps[:, 0:128], XN[:, k, tc_ * 128:(tc_ + 1) * 128], gvv[:, k, 0:128], start=(k == 0), stop=(k == KC - 1))
                    return ins
                op("pe", mmv, r=[gvv[:, :, 0:128], XN[:, :, tc_ * 128:(tc_ + 1) * 128]], w=[ps[:, :]])
                op("act", lambda e, ps=ps, tc_=tc_: e.activation(VP[:, tc_, 0, 0:64], ps[:, 0:64], AF.Copy),
                   r=[ps[:, :]], w=[VP[:, tc_, 0, :]])
                op("act", lambda e, ps=ps, tc_=tc_: e.activation(VP[:, tc_, 1, 64:128], ps[:, 64:128], AF.Copy),
                   r=[ps[:, :]], w=[VP[:, tc_, 1, :]])
        if DBG["attn_level"] < 1:
            return
        for t in range(NT):
            proj_tile(t)
        if DBG["attn_level"] < 2:
            return
        if kind == "moba":
            op("dve", lambda e: e.tensor_reduce(KM[:, :], QK[:, 1, :].rearrange("p (a b) -> p a b", a=8), AX.X, ALU.add),
               r=[QK[:, 1, :]], w=[KM[:, :]])
            op("dve", lambda e: e.tensor_scalar(KMB[:, :], KM[:, :], 1.0 / 256, None, ALU.mult), r=[KM[:, :]], w=[KMB[:, :]])
            pgate = bank()

            def mmgate(e):
                for h in range(2):
                    for qc in range(16):
                        c0 = (h * 16 + qc) * 8
                        ins = e.matmul(pgate[:, c0:c0 + 8], QK[h * 64:(h + 1) * 64, 0, qc * 128:(qc + 1) * 128],
                                       KMB[h * 64:(h + 1) * 64, :], start=True, stop=True)
                return ins
            op("pe", mmgate, r=[QK[:, 0, :], KMB[:, :]], w=[pgate[:, :]])
            for h in range(2):
                op("dve", lambda e, h=h: e.tensor_tensor(
                    GM[:, h, :, :], pgate[:, h * 128:(h + 1) * 128].rearrange("p (a b) -> p a b", a=16), NEGM[:, :, :], ALU.add),
                   r=[pgate[:, :], NEGM[:, :, :]], w=[GM[:, h, :, :]])
            for h in range(2):
                for qc in range(16):
                    op("dve", lambda e, h=h, qc=qc: e.max(MX8[:, h, qc, :], GM[:, h, qc, :]),
                       r=[GM[:, h, qc, :]], w=[MX8[:, h, qc, :]])
            for h in range(2):
                for qc in range(16):
                    op("dve", lambda e, h=h, qc=qc: e.tensor_scalar(BI[:, h, qc, :], GM[:, h, qc, :], MX8[:, h, qc, 2:3], NEG,
                                                                    ALU.is_lt, ALU.mult),
                       r=[GM[:, h, qc, :], MX8[:, h, qc, :]], w=[BI[:, h, qc, :]])
            for h in range(2):
                for q4 in range(4):
                    pt = bank()

                    def tr(e, pt=pt, h=h, q4=q4):
                        for j in range(4):
                            ins = e.transpose(pt[0:8, j * 128:(j + 1) * 128], BI[:, h, q4 * 4 + j, :], IDF[:, :])
                        return ins
                    op("pe", tr, r=[BI[:, h, :, :], IDF[:, :]], w=[pt[:, :]])
                    op("act", lambda e, pt=pt, h=h, q4=q4: e.activation(BIAST[0:8, h, q4 * 512:(q4 + 1) * 512], pt[0:8, :], AF.Copy),
                       r=[pt[:, :]], w=[BIAST[:, h, q4 * 512:(q4 + 1) * 512]])
        if DBG["attn_level"] < 3:
            return
        ecnt = [0]

        def sweep_tile(t):
            pnum = banks[6]
            pden = banks[7]
            nck = 4 * t + 4
            first = True
            for c in range(nck):
                k0 = c * 128
                q0 = max(k0, t * TT)
                q1 = (t + 1) * TT
                n = c // 2
                for h in range(2):
                    hs = slice(h * 64, (h + 1) * 64)
                    pS = bank()
                    Et = ET[ecnt[0] % 4]
                    ecnt[0] += 1
                    w_ = q1 - q0

                    def mms(e, pS=pS, hs=hs, h=h, k0=k0, q0=q0, q1=q1, n=n, w_=w_):
                        extra = []
                        if q0 == k0:
                            extra.append(("tri", 0, 128))
                        if kind == "moba":
                            ob = max(q0, 256 * (n + 1))
                            if ob < q1:
                                extra.append(("sel", ob - q0, q1 - q0, ob, q1))
                        ins = e.matmul(pS[:, 0:w_], QK[hs, 1, k0:k0 + 128], QK[hs, 0, q0:q1], start=True, stop=(len(extra) == 0))
                        for xi, x in enumerate(extra):
                            lastx = (xi == len(extra) - 1)
                            if x[0] == "tri":
                                ins = e.matmul(pS[:, 0:128], IDB[:, :], TRI[:, :], start=False, stop=lastx)
                            else:
                                ins = e.matmul(pS[:, x[1]:x[2]], ESEL[0:8, n, :], BIAST[0:8, h, x[3]:x[4]], start=False, stop=lastx)
                        return ins
                    rr = [QK[:, :, k0:k0 + 128], QK[:, 0, q0:q1], IDB[:, :], TRI[:, :], ESEL[0:8, :, :]]
                    if kind == "moba":
                        rr.append(BIAST[:, h, q0:q1])
                    op("pe", mms, r=rr, w=[pS[:, :]])
                    op("act", lambda e, pS=pS, Et=Et, w_=w_: e.activation(Et[:, 0:w_], pS[:, 0:w_], AF.Exp, scale=0.125),
                       r=[pS[:, :]], w=[Et[:, :]])
                    if kind == "dil":
                        dlt = q0 - k0
                        op("dve", lambda e, Et=Et, w_=w_, dlt=dlt: e.tensor_tensor(Et[:, 0:w_], Et[:, 0:w_], DM[:, dlt:dlt + w_], ALU.mult),
                           r=[Et[:, :], DM[:, dlt:dlt + w_]], w=[Et[:, :]])
                    lastpv = (c == nck - 1 and h == 1)

                    def mmpv(e, Et=Et, w_=w_, q0=q0, c=c, h=h, first=first, lastpv=lastpv):
                        o0 = q0 - t * TT
                        e.matmul(pnum[:, o0:o0 + w_], VP[:, c, h, :], Et[:, 0:w_], start=first, stop=lastpv)
                        return e.matmul(pden[:, o0:o0 + w_], ONEH[:, h, :], Et[:, 0:w_], start=first, stop=lastpv)
                    op("pe", mmpv, r=[VP[:, c, h, :], Et[:, :], ONEH[:, :, :]], w=[pnum[:, :], pden[:, :]])
                    first = False
            rd = TMP[t % 2]
            op("dve", lambda e, rd=rd, pden=pden: e.reciprocal(rd[:, :], pden[:, :]), r=[pden[:, :]], w=[rd[:, :]])
            op("dve", lambda e, rd=rd, pnum=pnum: e.tensor_tensor(YA[:, tl(t)], pnum[:, :], rd[:, :], ALU.mult),
               r=[pnum[:, :], rd[:, :]], w=[YA[:, tl(t)]])
            wout_apply(wo, 1, lambda c: YA[:, tl(t)], t)
        for t in range(NT):
            sweep_tile(t)

    rot_d = din("rot", [128, 128])

    def final_tile(s, t):
        if final_norm:
            ps = bank()
            for c in range(KC):
                sq = SQ[c % 2]
                op("act", lambda e, c=c, sq=sq: e.activation(sq[:, :], X[:, c, tl(t)], AF.Square),
                   r=[X[:, c, tl(t)]], w=[sq[:, :]])
                op("pe", lambda e, c=c, sq=sq: e.matmul(ps[:, :], ONES[:, :], sq[:, :], start=(c == 0), stop=(c == KC - 1)),
                   r=[sq[:, :], ONES[:, :]], w=[ps[:, :]])
            op("act", lambda e: e.activation(RSTD[:, :], ps[:, :], AF.Sqrt, bias=pvc(PV_EPS1), scale=1.0 / D),
               r=[ps[:, :], PV[:, :]], w=[RSTD[:, :]])
            op("dve", lambda e: e.reciprocal(RSTD[:, :], RSTD[:, :]), r=[RSTD[:, :]], w=[RSTD[:, :]])
        for c in range(KC):
            ot = TMP[c % 4]
            if final_norm:
                op("dve", lambda e, c=c, ot=ot: e.scalar_tensor_tensor(ot[:, :], X[:, c, tl(t)], pvc(PV_NF + c), RSTD[:, :],
                                                                      ALU.mult, ALU.mult),
                   r=[X[:, c, tl(t)], RSTD[:, :], PV[:, :]], w=[ot[:, :]])
                src = ot[:, :]
            else:
                src = X[:, c, tl(t)]
            op("sp", lambda e, c=c, src=src: e.dma_start(out=outT[s, c * 128:(c + 1) * 128, tl(t)], in_=src),
               r=[src], dma=True, out_dma=True)

    for s in range(nseq):
        for c in range(KC):
            op("sp", lambda e, c=c, s=s: e.dma_start(out=X[:, c, :], in_=xT[s, c * 128:(c + 1) * 128, :]),
               w=[X[:, c, :]], dma=True)
        for l in range(nlayers):
            if "ffn1" in stages:
                ffn(l, 0)
            if "mix" in stages:
                for t in range(NT):
                    rmsnorm_tile(PV_NM + l * 8, t, XN[:, :, tl(t)])
                if "pool" in mix_parts:
                    mix_pool(l)
                if "conv" in mix_parts:
                    mix_conv(l)
                if "moba" in mix_parts or "dil" in mix_parts:
                    rope_tables(s)
                    op("pool", lambda e: e.dma_start(out=ROT[:, :], in_=rot_d), w=[ROT[:, :]], dma=True)
                if "moba" in mix_parts:
                    for i in range(2):
                        mix_attn(l, "moba", i)
                if "dil" in mix_parts:
                    op("pool", lambda e: e.dma_start(out=DM[:, :], in_=dmask_d[0:128, :]), w=[DM[:, :]], dma=True)
                    for i in range(2):
                        mix_attn(l, "dil", i)
            if "ffn2" in stages:
                ffn(l, 1)
        for t in range(NT):
            final_tile(s, t)

    sc.emit(nc, es)
    es.close()
    return nc, len(sc.ops)


def _consts():
    ident = np.eye(128, dtype=np.float32)
    i = np.arange(128)[:, None]
    j = np.arange(128)[None, :]
    tri = np.where(i <= j, 0.0, NEG).astype(np.float32)
    x = np.arange(S)[None, :]
    dd = x - i
    mult = ((dd >= 0) & (dd <= 128)).astype(np.float32) \
        + ((dd >= 0) & (dd <= 512) & (dd % 4 == 0)).astype(np.float32) \
        + ((dd >= 0) & (dd <= 2048) & (dd % 16 == 0)).astype(np.float32)
    negm = np.zeros((128, 16, 8), np.float32)
    for qc in range(16):
        negm[:, qc, qc // 2:] = -1e30
    esel = np.zeros((8, 8, 128), np.float32)
    for n in range(8):
        esel[n, n, :] = 1.0
    oneh = np.zeros((128, 2, 128), np.float32)
    oneh[:, 0, 0:64] = 1.0
    oneh[:, 1, 64:128] = 1.0
    icnt = np.zeros((128, 2, 16), np.float32)
    wins = {(0, 0): 2, (0, 1): 4, (1, 0): 8, (1, 1): 16}
    for (c, h), w in wins.items():
        icnt[h * 64:(h + 1) * 64, c, :] = 1.0 / np.minimum(np.arange(16) + 1.0, float(w))
    rot = np.zeros((128, 128), np.float32)
    for hb in (0, 64):
        for d in range(8):
            rot[hb + d + 8, hb + d] = -1.0
            rot[hb + d, hb + d + 8] = 1.0
    return dict(ident=ident, tri=tri, dmask=mult.astype(np.float32), negm=negm.reshape(128, 128),
                esel=esel.reshape(8, 1024), oneh=oneh.reshape(128, 256), icnt=icnt.reshape(128, 32), rot=rot)


def _pvec(inp):
    pv = np.zeros((128, PV_NCOL), np.float32)

    def fm(v):
        return np.ascontiguousarray(np.asarray(v, np.float32).reshape(-1, 128).T)
    for l in range(NL):
        pv[:, PV_N1 + l * 8:PV_N1 + l * 8 + 8] = fm(inp["ffn1_norm"][l])
        pv[:, PV_NM + l * 8:PV_NM + l * 8 + 8] = fm(inp["mix_norm"][l])
        pv[:, PV_N2 + l * 8:PV_N2 + l * 8 + 8] = fm(inp["ffn2_norm"][l])
        pv[:, PV_PS + l * 2:PV_PS + l * 2 + 2] = fm(inp["pool_scale"][l])
        cw = np.asarray(inp["conv_w"][l], np.float32)
        for c in range(2):
            pv[:, PV_CW + (l * 2 + c) * 31:PV_CW + (l * 2 + c + 1) * 31] = cw[:, c * 128:(c + 1) * 128].T
        pv[:, PV_CB + l * 2:PV_CB + l * 2 + 2] = fm(inp["conv_b"][l])
        pv[:, PV_LG + l * 2:PV_LG + l * 2 + 2] = fm(inp["conv_ln_g"][l])
        pv[:, PV_LB + l * 2:PV_LB + l * 2 + 2] = fm(inp["conv_ln_b"][l])
    pv[:, PV_NF:PV_NF + 8] = fm(inp["final_norm"])
    inv = (np.float32(500000.0) ** (-np.arange(0, 16, 2, dtype=np.float32) / np.float32(16))).astype(np.float32)
    p = np.arange(128) % 64
    pv[:, PV_INV] = np.where(p < 16, inv[p % 8], 0.0)
    pv[:, PV_INV + 3] = np.float32(np.pi / 2)
    pv[:, PV_EPS1] = 1e-6
    pv[:, PV_EPS2] = 1e-5
    pv[0:64, PV_PIW + 0] = 1.0 / 2
    pv[64:128, PV_PIW + 0] = 1.0 / 4
    pv[0:64, PV_PIW + 1] = 1.0 / 8
    pv[64:128, PV_PIW + 1] = 1.0 / 16
    return pv


_CACHE = {}
SEQ_PER_LAUNCH = 1
CORES_PER_LAUNCH = 2


def kernel(**inputs):
    inp = {k: np.asarray(v) for k, v in inputs.items()}
    x = inp["x"].astype(np.float32, copy=False)
    B = x.shape[0]
    assert B == NCORES * NSEQ
    if "nc" not in _CACHE:
        _CACHE["nc"] = build_program(nseq=SEQ_PER_LAUNCH)[0]
    nc = _CACHE["nc"]
    cst = _consts()
    pv = _pvec(inp)
    shared = {k: np.ascontiguousarray(inp[k], dtype=np.float32) for k in
              ("ffn1_gate", "ffn1_up", "ffn1_down", "ffn2_gate", "ffn2_up", "ffn2_down", "w_in", "w_out", "pool_w")}
    shared.update(cst)
    shared["pvec"] = pv
    out = np.empty((B, S, D), np.float32)
    for s0 in range(0, NSEQ, SEQ_PER_LAUNCH):
        in_maps = []
        for c in range(NCORES):
            b0 = c * NSEQ + s0
            xs = x[b0:b0 + SEQ_PER_LAUNCH]
            m = dict(shared)
            m["xT"] = np.ascontiguousarray(xs.transpose(0, 2, 1))
            ps = inp["positions"][b0:b0 + SEQ_PER_LAUNCH].astype(np.int32)
            m["posb"] = np.ascontiguousarray(np.broadcast_to(ps[:, None, :], (SEQ_PER_LAUNCH, 128, S)))
            in_maps.append(m)
        for g0 in range(0, NCORES, CORES_PER_LAUNCH):
            res = run_bass_kernel_spmd(nc, in_maps[g0:g0 + CORES_PER_LAUNCH], core_ids=list(range(CORES_PER_LAUNCH)))
            for j in range(CORES_PER_LAUNCH):
                c = g0 + j
                o = res.results[j]["outT"]
                b0 = c * NSEQ + s0
                out[b0:b0 + SEQ_PER_LAUNCH] = o.transpose(0, 2, 1)
    return out
```
